# Optimizing a Trainium2 kernel written in Bass

```python
import jax, jax.numpy as jnp
from jax import lax
import numpy as np

D_MODEL = 2048
BATCH = 4
SEQ = 8192
DEPTH = 4

MLSTM_WIDTH = D_MODEL // 2
MLSTM_HEADS = 4
MLSTM_DV = MLSTM_WIDTH // MLSTM_HEADS
MLSTM_DQK = MLSTM_DV // 2
MLSTM_CONV = 4
MLSTM_CHUNK = 64
GATE_SOFTCAP = 15.0

ATTN_WIDTH = D_MODEL - MLSTM_WIDTH
ATTN_HEAD_DIM = 64
ATTN_HEADS = ATTN_WIDTH // ATTN_HEAD_DIM
ATTN_KV_HEADS = max(1, ATTN_HEADS // 8)
WINDOW = 128
ROPE_DIM = ATTN_HEAD_DIM // 4
ROPE_THETA = 500000.0

D_FF = ((11 * D_MODEL // 4) + 255) // 256 * 256
FFN_CONV = 3
NORM_EPS = 1e-6

MLSTM_QK_WIDTH = MLSTM_HEADS * MLSTM_DQK
ATTN_KV_WIDTH = ATTN_KV_HEADS * ATTN_HEAD_DIM
IN_WIDTHS = (MLSTM_QK_WIDTH, MLSTM_QK_WIDTH, MLSTM_WIDTH, MLSTM_WIDTH, MLSTM_HEADS, MLSTM_HEADS,
             ATTN_WIDTH, ATTN_KV_WIDTH, ATTN_KV_WIDTH)
IN_WIDTH = sum(IN_WIDTHS)

kernel_name = 'hymba_style_mlstm_swa_convglu_adaln'


def rms_norm(x, g):
    xf = x.astype(jnp.float32)
    out = xf * lax.rsqrt(jnp.mean(xf * xf, axis=-1, keepdims=True) + NORM_EPS)
    return (out * g.astype(jnp.float32)).astype(x.dtype)


def modulate(h, shift, scale):
    return h * (1.0 + scale[:, None, :]) + shift[:, None, :]


def causal_depthwise_conv(x, w):
    W = w.shape[0]
    S = x.shape[1]
    xp = jnp.pad(x, ((0, 0), (W - 1, 0), (0, 0)))
    w = w.astype(x.dtype)
    out = xp[:, 0:S] * w[0]
    for k in range(1, W):
        out = out + xp[:, k:k + S] * w[k]
    return out


def rope_tables(positions):
    inv_freq = ROPE_THETA ** (-jnp.arange(0, ROPE_DIM, 2, dtype=jnp.float32) / ROPE_DIM)
    ang = positions.astype(jnp.float32)[..., None] * inv_freq
    ang = jnp.concatenate([ang, ang], axis=-1)[:, :, None, :]
    return jnp.cos(ang), jnp.sin(ang)


def apply_partial_rope(t, cos, sin):
    tr, tp = t[..., :ROPE_DIM], t[..., ROPE_DIM:]
    t1, t2 = jnp.split(tr, 2, axis=-1)
    rot = jnp.concatenate([-t2, t1], axis=-1)
    tr = (tr.astype(jnp.float32) * cos + rot.astype(jnp.float32) * sin).astype(t.dtype)
    return jnp.concatenate([tr, tp], axis=-1)


def mlstm_chunkwise(q, k, v, i_pre, f_pre):
    B, S, H, DK = q.shape
    DV = v.shape[-1]
    L = MLSTM_CHUNK
    NC = S // L
    f32 = jnp.float32

    def to_chunks(t):
        t = t.astype(f32).reshape((B, NC, L, H) + t.shape[3:])
        return jnp.moveaxis(t, (1, 3), (0, 2))

    qc = to_chunks(q) * (DK ** -0.5)
    kc = to_chunks(k)
    vc = to_chunks(v)
    ic = to_chunks(i_pre)
    fc = to_chunks(jax.nn.log_sigmoid(f_pre.astype(f32)))
    causal = jnp.tril(jnp.ones((L, L), dtype=bool))

    def step(carry, xs):
        C, n, m = carry
        qj, kj, vj, ij, lf = xs
        b = jnp.cumsum(lf, axis=-1)
        d = b[..., :, None] - b[..., None, :] + ij[..., None, :]
        d = jnp.where(causal, d, -jnp.inf)
        m_inter = b + m[..., None]
        m_t = jnp.maximum(m_inter, d.max(axis=-1))
        w_intra = jnp.exp(d - m_t[..., None])
        w_inter = jnp.exp(m_inter - m_t)
        s = jnp.einsum('bhjd,bhrd->bhjr', qj, kj) * w_intra
        num = (jnp.einsum('bhjr,bhrv->bhjv', s, vj)
               + w_inter[..., None] * jnp.einsum('bhvd,bhjd->bhjv', C, qj))
        nq = s.sum(axis=-1) + w_inter * jnp.einsum('bhd,bhjd->bhj', n, qj)
        h = num / jnp.maximum(jnp.abs(nq), jnp.exp(-m_t))[..., None]
        b_last = b[..., -1]
        a = b_last[..., None] - b + ij
        m_new = jnp.maximum(b_last + m, a.max(axis=-1))
        w_state = jnp.exp(a - m_new[..., None])
        decay = jnp.exp(b_last + m - m_new)
        C_new = decay[..., None, None] * C + jnp.einsum('bhr,bhrv,bhrd->bhvd', w_state, vj, kj)
        n_new = decay[..., None] * n + jnp.einsum('bhr,bhrd->bhd', w_state, kj)
        return (C_new, n_new, m_new), h

    init = (jnp.zeros((B, H, DV, DK), f32), jnp.zeros((B, H, DK), f32), jnp.zeros((B, H), f32))
    _, hc = lax.scan(step, init, (qc, kc, vc, ic, fc))
    return jnp.moveaxis(hc, (0, 2), (1, 3)).reshape(B, S, H, DV).astype(v.dtype)


def sliding_window_attention(q, k, v, sinks):
    B, S, H, Dh = q.shape
    KVH = k.shape[2]
    G = H // KVH
    NB = S // WINDOW
    qb = q.reshape(B, NB, WINDOW, KVH, G, Dh)

    def band(t):
        tb = t.reshape(B, NB, WINDOW, KVH, Dh)
        prev = jnp.concatenate([jnp.zeros_like(tb[:, :1]), tb[:, :-1]], axis=1)
        return jnp.concatenate([prev, tb], axis=2)

    kb, vb = band(k), band(v)
    scores = jnp.einsum('bnqhgd,bnkhd->bnhgqk', qb, kb,
                        preferred_element_type=jnp.float32) * (Dh ** -0.5)
    qi = jnp.arange(WINDOW)[:, None]
    ki = jnp.arange(2 * WINDOW)[None, :]
    in_band = (ki > qi) & (ki <= qi + WINDOW)
    has_prev = (jnp.arange(NB) > 0)[:, None, None] | (ki >= WINDOW)[None]
    mask = in_band[None] & has_prev
    scores = jnp.where(mask[None, :, None, None], scores, -jnp.inf)
    sink = sinks.astype(jnp.float32).reshape(1, 1, KVH, G, 1, 1)
    m = jnp.maximum(scores.max(axis=-1, keepdims=True), sink)
    p = jnp.exp(scores - m)
    probs = p / (p.sum(axis=-1, keepdims=True) + jnp.exp(sink - m))
    out = jnp.einsum('bnhgqk,bnkhd->bnqhgd', probs.astype(v.dtype), vb)
    return out.reshape(B, S, H * Dh)


def hybrid_layer(x, mod, cos, sin, norm_mix_g, w_in, b_gates, conv_qk, mlstm_head_g, sinks,
                 w_out, norm_ffn_g, w_up, conv_ffn, w_down):
    B, S, _ = x.shape
    shift_m, scale_m, gate_m, shift_f, scale_f, gate_f = jnp.split(mod, 6, axis=-1)

    h = modulate(rms_norm(x, norm_mix_g), shift_m, scale_m)
    proj = h @ w_in
    split_points = np.cumsum(IN_WIDTHS)[:-1].tolist()
    q_m, k_m, v_m, o_m, i_pre, f_pre, q_a, k_a, v_a = jnp.split(proj, split_points, axis=-1)

    qk_m = jax.nn.silu(causal_depthwise_conv(jnp.concatenate([q_m, k_m], axis=-1), conv_qk))
    q_m, k_m = jnp.split(qk_m, 2, axis=-1)
    gates = jnp.concatenate([i_pre, f_pre], axis=-1).astype(jnp.float32) + b_gates.astype(jnp.float32)
    gates = GATE_SOFTCAP * jnp.tanh(gates / GATE_SOFTCAP)
    i_pre, f_pre = jnp.split(gates, 2, axis=-1)
    h_m = mlstm_chunkwise(q_m.reshape(B, S, MLSTM_HEADS, MLSTM_DQK),
                          k_m.reshape(B, S, MLSTM_HEADS, MLSTM_DQK),
                          v_m.reshape(B, S, MLSTM_HEADS, MLSTM_DV), i_pre, f_pre)
    h_m = rms_norm(h_m, mlstm_head_g.reshape(MLSTM_HEADS, MLSTM_DV))
    y_m = jax.nn.sigmoid(o_m) * h_m.reshape(B, S, MLSTM_WIDTH)

    q_a = apply_partial_rope(q_a.reshape(B, S, ATTN_HEADS, ATTN_HEAD_DIM), cos, sin)
    k_a = apply_partial_rope(k_a.reshape(B, S, ATTN_KV_HEADS, ATTN_HEAD_DIM), cos, sin)
    v_a = v_a.reshape(B, S, ATTN_KV_HEADS, ATTN_HEAD_DIM)
    y_a = sliding_window_attention(q_a, k_a, v_a, sinks)

    y = jnp.concatenate([y_m, y_a], axis=-1) @ w_out
    x = x + gate_m[:, None, :] * y

    h = modulate(rms_norm(x, norm_ffn_g), shift_f, scale_f)
    g, u = jnp.split(h @ w_up, 2, axis=-1)
    g = causal_depthwise_conv(g, conv_ffn)
    y = (jax.nn.silu(g) * u) @ w_down
    return x + gate_f[:, None, :] * y


def setup_inputs(seed: int = 0) -> dict:
    key = jax.random.key(seed)
    ks = jax.random.split(key, 20)
    f32 = jnp.float32

    def nrm(k, shape, scale):
        return jax.random.normal(k, shape, f32) * scale

    x = nrm(ks[0], (BATCH, SEQ, D_MODEL), 1.0)
    c = nrm(ks[1], (BATCH, D_MODEL), 1.0)
    positions = (jax.random.randint(ks[2], (BATCH, 1), 0, 4096, dtype=jnp.int32)
                 + jnp.arange(SEQ, dtype=jnp.int32)[None, :])
    ada_w = nrm(ks[3], (DEPTH, D_MODEL, 6 * D_MODEL), 0.5 * D_MODEL ** -0.5)
    ada_b = nrm(ks[4], (DEPTH, 6 * D_MODEL), 0.02)
    norm_mix_g = 1.0 + nrm(ks[5], (DEPTH, D_MODEL), 0.02)
    w_in = nrm(ks[6], (DEPTH, D_MODEL, IN_WIDTH), D_MODEL ** -0.5)
    f_bias = jnp.linspace(3.0, 6.0, MLSTM_HEADS, dtype=f32)
    b_gates = jnp.concatenate([nrm(ks[7], (DEPTH, MLSTM_HEADS), 0.1),
                               f_bias + nrm(ks[8], (DEPTH, MLSTM_HEADS), 0.1)], axis=-1)
    conv_qk = nrm(ks[9], (DEPTH, MLSTM_CONV, 2 * MLSTM_QK_WIDTH), MLSTM_CONV ** -0.5)
    mlstm_head_g = 1.0 + nrm(ks[10], (DEPTH, MLSTM_WIDTH), 0.02)
    sinks = nrm(ks[11], (DEPTH, ATTN_HEADS), 0.5)
    w_out = nrm(ks[12], (DEPTH, MLSTM_WIDTH + ATTN_WIDTH, D_MODEL), (MLSTM_WIDTH + ATTN_WIDTH) ** -0.5)
    norm_ffn_g = 1.0 + nrm(ks[13], (DEPTH, D_MODEL), 0.02)
    w_up = nrm(ks[14], (DEPTH, D_MODEL, 2 * D_FF), D_MODEL ** -0.5)
    conv_ffn = nrm(ks[15], (DEPTH, FFN_CONV, D_FF), FFN_CONV ** -0.5)
    w_down = nrm(ks[16], (DEPTH, D_FF, D_MODEL), D_FF ** -0.5)
    final_g = 1.0 + nrm(ks[17], (D_MODEL,), 0.02)
    return {'x': x, 'c': c, 'positions': positions, 'ada_w': ada_w, 'ada_b': ada_b,
            'norm_mix_g': norm_mix_g, 'w_in': w_in, 'b_gates': b_gates, 'conv_qk': conv_qk,
            'mlstm_head_g': mlstm_head_g, 'sinks': sinks, 'w_out': w_out,
            'norm_ffn_g': norm_ffn_g, 'w_up': w_up, 'conv_ffn': conv_ffn, 'w_down': w_down,
            'final_g': final_g}


def reference(x, c, positions, ada_w, ada_b, norm_mix_g, w_in, b_gates, conv_qk, mlstm_head_g,
              sinks, w_out, norm_ffn_g, w_up, conv_ffn, w_down, final_g):
    cos, sin = rope_tables(positions)
    c_act = jax.nn.silu(c)
    for l in range(DEPTH):
        mod = c_act @ ada_w[l] + ada_b[l]
        x = hybrid_layer(x, mod, cos, sin, norm_mix_g[l], w_in[l], b_gates[l], conv_qk[l],
                         mlstm_head_g[l], sinks[l], w_out[l], norm_ffn_g[l], w_up[l],
                         conv_ffn[l], w_down[l])
    return rms_norm(x, final_g)
```

```python
import math
import os
from contextlib import ExitStack

import numpy as np
import concourse.bass as bass
import concourse.mybir as mybir
from concourse.bass_utils import run_bass_kernel_spmd

F32 = mybir.dt.float32
BF16 = mybir.dt.bfloat16
I32 = mybir.dt.int32
AF = mybir.ActivationFunctionType
ALU = mybir.AluOpType
AX = mybir.AxisListType

D = 2048
KC = 16
T = 512
NBLK = 4
DFF = 5632
NFC = 44
INW = 4360
EPS = 1e-6
NEG = -30000.0
QS = 128.0 ** -0.5
FFN_GROUPS = [(0, 12), (12, 24), (24, 34), (34, 44)]
NSLOT = 4
import os
STOP = float(os.environ.get('KSTOP', '9'))
TWO_PI = 2.0 * math.pi

C_IDF = 0
C_PERM = 128
C_MMASK = 256
C_SEL = 768
C_INVF = 1280
NCONST = 1281
C_AMASK = 1281
NCONST_D = 1793


def make_consts():
    c = np.zeros((128, NCONST_D), np.float32)
    c[:, C_IDF:C_IDF + 128] = np.eye(128, dtype=np.float32)
    pm = np.zeros((128, 128), np.float32)
    for hb in (0, 64):
        for f in range(8):
            pm[hb + f + 8, hb + f] = -1.0
            pm[hb + f, hb + f + 8] = 1.0
    c[:, C_PERM:C_PERM + 128] = pm
    k = np.arange(128)[:, None]
    q = np.arange(128)[None, :]
    prev = np.where(k > q, 0.0, NEG).astype(np.float32)
    cur = np.where(k <= q, 0.0, NEG).astype(np.float32)
    c[:, C_AMASK:C_AMASK + 512] = np.concatenate([prev, cur, prev, cur], axis=1)
    mm = np.where(k > q, float(os.environ.get('KBIG', '60.0')), 0.0).astype(np.float32)
    c[:, C_MMASK:C_MMASK + 512] = np.concatenate([mm] * 4, axis=1)
    for h in range(4):
        c[h, C_SEL + h * 128:C_SEL + (h + 1) * 128] = 1.0
    inv = (500000.0 ** (-np.arange(0, 16, 2, dtype=np.float32) / 16.0)).astype(np.float32)
    for p in range(128):
        f = p % 64
        c[p, C_INVF] = inv[f % 8] if f < 16 else 0.0
    return c


class Prog:
    def __init__(self):
        self.ins = []
        self.lastw = {}
        self.readers = {}
        self.dma_count = {}
        self.alias = {}
        self.last_any = {}

    def _x(self, keys):
        out = []
        for k in keys:
            a = self.alias.get(k)
            if a is None:
                out.append(k)
            else:
                out.extend(a)
        return out

    def add(self, eng, fn, r=(), w=(), dma=None, w_nodep=()):
        r, w, w_nodep = self._x(r), self._x(w), self._x(w_nodep)
        def bank(k):
            return k[:3] if (len(k) >= 3 and k[:2] == "ps" and k[2].isdigit()) else k
        r = list(dict.fromkeys(bank(k) for k in r))
        w = list(dict.fromkeys(bank(k) for k in w))
        i = len(self.ins)
        deps = set()
        for k in set(r) | set(w):
            if len(k) == 3 and k[:2] == "ps" and k[2].isdigit():
                la = self.last_any.setdefault(k, {})
                for e2, j in la.items():
                    if e2 != eng:
                        deps.add(j)
                la[eng] = i
        for k in r:
            if k in self.lastw:
                deps.add(self.lastw[k])
        for k in w:
            if k in self.lastw:
                deps.add(self.lastw[k])
            deps.update(self.readers.get(k, ()))
        rec = dict(eng=eng, fn=fn, deps=deps, dma=dma, cnt=None, needed=False, idx=None)
        if dma is not None:
            self.dma_count[dma] = self.dma_count.get(dma, 0) + 1
            rec['cnt'] = self.dma_count[dma]
        self.ins.append(rec)
        for k in r:
            self.readers.setdefault(k, []).append(i)
        for k in list(w) + list(w_nodep):
            self.lastw[k] = i
            self.readers[k] = []
        return i

    def emit(self, nc, es):
        ins = self.ins
        for rec in ins:
            for d in rec['deps']:
                dd = ins[d]
                if dd['dma'] is None and not (dd['eng'] == 'pe' and rec['eng'] == 'pe'):
                    dd['needed'] = True
        cnt = {}
        per_eng = {}
        for rec in ins:
            per_eng.setdefault(rec['eng'], []).append(rec)
            if rec['dma'] is None and rec['needed']:
                cnt[rec['eng']] = cnt.get(rec['eng'], 0) + 1
                rec['idx'] = cnt[rec['eng']]
        esem = {e: es.enter_context(nc.semaphore("se_" + e)) for e in per_eng}
        dsem = {}
        for n, k in enumerate(self.dma_count):
            dsem[k] = es.enter_context(nc.semaphore("sd_%d" % n))
        block = es.enter_context(nc.Block())

        def run(ename, eng):
            waited = {}
            for rec in per_eng.get(ename, []):
                waits = {}
                for d in rec['deps']:
                    dd = ins[d]
                    if dd['dma'] is not None:
                        key = ('d', dd['dma'])
                        val = 16 * dd['cnt']
                    else:
                        if dd['eng'] == 'pe' and ename == 'pe':
                            continue
                        key = ('e', dd['eng'])
                        val = dd['idx']
                    if val > waits.get(key, 0):
                        waits[key] = val
                for key, val in waits.items():
                    if waited.get(key, 0) >= val:
                        continue
                    eng.wait_ge(dsem[key[1]] if key[0] == 'd' else esem[key[1]], val)
                    waited[key] = val
                if rec['fn'] is None:
                    continue
                inst = rec['fn'](eng)
                if rec['dma'] is not None:
                    inst.then_inc(dsem[rec['dma']], 16)
                elif rec['needed']:
                    inst.then_inc(esem[ename], 1)

        @block.sync
        def _(e):
            run('sp', e)

        @block.scalar
        def _(e):
            run('act', e)

        @block.vector
        def _(e):
            run('dve', e)

        @block.gpsimd
        def _(e):
            run('pool', e)

        @block.tensor
        def _(e):
            run('pe', e)


def build_program(L, NT, S_total=None):
    S = NT * T
    nc = bass.Bass("TRN2", target_bir_lowering=False)
    es = ExitStack()
    P = Prog()

    def dram(name, shape, dt, kind):
        return nc.dram_tensor(name, list(shape), dt, kind=kind)

    x_d = dram("x", [S, D], F32, "ExternalInput").ap()
    c_d = dram("c", [KC, 128], F32, "ExternalInput").ap()
    pos_d = dram("positions", [1, S], I32, "ExternalInput").ap()
    adaw_d = dram("ada_w", [L, D, 6 * D], F32, "ExternalInput").ap()
    adab_d = dram("ada_b", [L, 96, 128], F32, "ExternalInput").ap()
    nmix_d = dram("norm_mix_g", [L, KC, 128], F32, "ExternalInput").ap()
    win_d = dram("w_in", [L, D, INW], F32, "ExternalInput").ap()
    bg_d = dram("b_gates", [L, 8, 1], F32, "ExternalInput").ap()
    cqk_d = dram("conv_qk", [L, 32, 128], F32, "ExternalInput").ap()
    hg_d = dram("mlstm_head_g", [L, 8, 128], F32, "ExternalInput").ap()
    sink_d = dram("sinks", [L, 1, 16], F32, "ExternalInput").ap()
    wout_d = dram("w_out", [L, D, D], F32, "ExternalInput").ap()
    nffn_d = dram("norm_ffn_g", [L, KC, 128], F32, "ExternalInput").ap()
    wup_d = dram("w_up", [L, D, 2 * DFF], F32, "ExternalInput").ap()
    cffn_d = dram("conv_ffn", [L, 132, 128], F32, "ExternalInput").ap()
    wdn_d = dram("w_down", [L, DFF, D], F32, "ExternalInput").ap()
    fg_d = dram("final_g", [KC, 128], F32, "ExternalInput").ap()
    cst_d = dram("consts", [128, NCONST_D], F32, "ExternalInput").ap()
    out_d = dram("out", [S, D], F32, "ExternalOutput").ap()

    WI = dram("s_wi", [L, 18, 128, KC, 256], BF16, "Internal").ap()
    WO = dram("s_wo", [L, 8, 128, KC, 256], BF16, "Internal").ap()
    WU = dram("s_wu", [L, 44, 128, KC, 256], BF16, "Internal").ap()
    WD = dram("s_wd", [L, 4, 8, 128, 12, 256], BF16, "Internal").ap()

    def sb(name, shape, dt):
        return es.enter_context(nc.sbuf_tensor(name, list(shape), dt))

    def psum(name, shape, dt):
        return es.enter_context(nc.psum_tensor(name, list(shape), dt))

    xT = sb("xT", [128, KC, T], F32)
    hT = sb("hT", [128, KC, T], BF16)
    wslot = [sb("wslot%d" % i, [128, KC * 256], BF16) for i in range(NSLOT)]
    cst = sb("cst", [128, NCONST], F32)
    idb = sb("idb", [128, 128], BF16)
    onesb = sb("onesb", [128, 128], BF16)
    amaskb = sb("amaskb", [128, 512], BF16)
    sqb = [sb("sqb%d" % i, [128, T], BF16) for i in range(2)]
    ft = [sb("ft%d" % i, [128, T + 4], F32) for i in range(8)]
    modp = sb("modp", [128, L, 96], F32)
    PA = sb("PA", [128, 128], F32)
    PB1 = sb("PB1", [128, L, 128], F32)
    PB2 = sb("PB2", [128, L, 44], F32)
    pstage = sb("pstage", [128, 128], F32)
    cact = sb("cact", [128, KC], F32)
    fgc = sb("fgc", [128, KC], F32)
    esink = sb("esink", [128, L, 16], F32)
    gbias = sb("gbias", [4, L, 2], F32)
    qk_tail = sb("qk_tail", [128, L, 8, 3], F32)
    qkT = sb("qkT", [128, 8, T], BF16)
    vm = sb("vm", [128, NBLK, 4, 260], BF16)
    oT = sb("oT", [128, 8, T], BF16)
    kaT = sb("kaT", [128, 2, T + 128], BF16)
    ka_tail = sb("ka_tail", [128, L, 2, 128], BF16)
    va = sb("va", [128, NBLK + 1, 2, 68], BF16)
    va_tail = sb("va_tail", [128, L, 2, 68], BF16)
    cosT = sb("cosT", [128, T], F32)
    sinT = sb("sinT", [128, T], F32)
    g_G = sb("g_G", [4, T], F32)
    rows = sb("rows", [96, T], F32)
    gstate = sb("gstate", [4, L, 2], F32)
    gcols = sb("gcols", [128, NBLK, 96], F32)
    ecols = sb("ecols", [128, NBLK, 4], F32)
    gend = sb("gend", [128, 4, 5], F32)
    gprev = sb("gprev", [128, L, 4], F32)
    wexp = [sb("wexp%d" % i, [128, 128], F32) for i in range(2)]
    stw = [sb("stw%d" % i, [128, 128], BF16) for i in range(2)]
    mtmp = [sb("mtmp%d" % i, [128, 260], F32) for i in range(2)]
    hun = [sb("hun%d" % i, [128, 260], F32) for i in range(2)]
    sqj = sb("sqj", [128, 256], F32)
    mcol = [sb("mcol%d" % i, [128, 12], F32) for i in range(2)]
    vw = [sb("vw%d" % i, [128, 260], BF16) for i in range(2)]
    ktok = [sb("ktok%d" % i, [128, 128], BF16) for i in range(2)]
    Cst = sb("Cst", [128, L, 4, 260], F32)
    Cb = sb("Cb", [128, 4, 260], BF16)
    ytm = [sb("ytm%d" % i, [128, 256], BF16) for i in range(2)]
    yta = sb("yta", [128, 1024], BF16)
    pT = [sb("pT%d" % i, [128, 512], BF16) for i in range(2)]
    acol = [sb("acol%d" % i, [128, 8], F32) for i in range(2)]
    ffn_tail = sb("ffn_tail", [128, L, NFC, 2], F32)
    actb = sb("actb", [128, 12, T], BF16)
    qaT = actb
    xstage = hT[:].rearrange("p a b -> p (a b)").bitcast(F32)[:, 0:D]
    XS = ["h%d" % i for i in range(8)]
    rs_tmp, rstd = ft[4][:, 0:T], ft[5][:, 0:T]
    tmpn = [ft[6][:, 0:T], ft[7][:, 0:T]]
    qktmp = [ft[0][:, 0:T + 3], ft[1][:, 0:T + 3]]
    cacc = [ft[2][:, 0:T], ft[3][:, 0:T]]
    tmpq = [ft[0][:, 0:T], ft[1][:, 0:T]]
    rt1, rt2 = ft[2][:, 0:T], ft[3][:, 0:T]
    posi = ft[7][:, 0:T].bitcast(I32)
    ang = ft[6][:, 0:T]
    g_ti, g_tf, g_e, g_lfh, g_B = (ft[i][0:4, 0:T] for i in range(5))
    gbuf = [ft[0][:, 0:T + 2], ft[1][:, 0:T + 2]]
    ffacc = [ft[2][:, 0:T], ft[3][:, 0:T]]
    ffs = [ft[4][:, 0:T], ft[5][:, 0:T]]
    gbuf4 = [ft[i][:, 0:T + 2] for i in (0, 1, 4, 6)]
    ffacc4 = [ft[i][:, 0:T] for i in (2, 3, 5, 7)]
    AL = P.alias
    AL.update({"rs_tmp": ["ft4"], "rstd": ["ft5"], "tmpn0": ["ft6"], "tmpn1": ["ft7"],
               "qktmp0": ["ft0"], "qktmp0t": ["ft0"], "qktmp1": ["ft1"], "qktmp1t": ["ft1"],
               "cacc0": ["ft2"], "cacc1": ["ft3"], "tmpq0": ["ft0"], "tmpq1": ["ft1"],
               "rt1": ["ft2"], "rt2": ["ft3"], "posi": ["ft7"], "ang": ["ft6"],
               "g_ti": ["ft0"], "g_tf": ["ft1"], "g_e": ["ft2"], "g_lfh": ["ft3"], "g_B": ["ft4"],
               "gbuf0": ["ft0"], "gbuf0t": ["ft0"], "gbuf1": ["ft1"], "gbuf1t": ["ft1"],
               "gbuf2": ["ft4"], "gbuf2t": ["ft4"], "gbuf3": ["ft6"], "gbuf3t": ["ft6"],
               "ffacc0": ["ft2"], "ffacc1": ["ft3"], "ffacc2": ["ft5"], "ffacc3": ["ft7"],
               "xstage": XS})
    for i in range(8):
        AL["qaT%d" % i] = ["actb%d" % i]

    ps = [psum("ps%d" % i, [128, 512], F32) for i in range(8)]

    def pe(fn, r, w):
        P.add('pe', fn, r, w)

    def act(fn, r, w):
        P.add('act', fn, r, w)

    def dve(fn, r, w):
        P.add('dve', fn, r, w)

    def pool(fn, r, w):
        P.add('pool', fn, r, w)

    def mm(out, lhsT, rhs, start, stop, r, w):
        pe(lambda e: e.matmul(out, lhsT=lhsT, rhs=rhs, start=start, stop=stop), r, w)

    def tr(out, in_, ident, r, w):
        pe(lambda e: e.transpose(out, in_, ident), r, w)

    def A_(out, in_, func, bias=None, scale=None, r=(), w=()):
        kw = {}
        if bias is not None:
            kw['bias'] = bias
        if scale is not None:
            kw['scale'] = scale
        act(lambda e: e.activation(out, in_, func, **kw), r, w)

    def TS(out, in0, s1, s2, op0, op1=None, r=(), w=(), eng='dve'):
        if op1 is None:
            P.add(eng, lambda e: e.tensor_scalar(out, in0, s1, None, op0), r, w)
        else:
            P.add(eng, lambda e: e.tensor_scalar(out, in0, s1, s2, op0, op1), r, w)

    def TT(out, in0, in1, op, r=(), w=(), eng='dve'):
        P.add(eng, lambda e: e.tensor_tensor(out, in0, in1, op), r, w)

    def STT(out, in0, scalar, in1, op0, op1, r=(), w=()):
        dve(lambda e: e.scalar_tensor_tensor(out, in0, scalar, in1, op0, op1), r, w)

    def CP(out, in_, r=(), w=(), eng='dve'):
        if eng == 'act':
            act(lambda e: e.copy(out, in_), r, w)
        else:
            P.add(eng, lambda e: e.tensor_copy(out, in_), r, w)

    idf = cst[:, C_IDF:C_IDF + 128]
    perm = cst[:, C_PERM:C_PERM + 128]
    mmask = cst[:, C_MMASK:C_MMASK + 512]
    invf = cst[:, C_INVF:C_INVF + 1]

    slot_ctr = [0]

    def wload(src_ap, shape3, queue='sp'):
        i = slot_ctr[0] % NSLOT
        slot_ctr[0] += 1
        a, b = shape3
        view = wslot[i][:, 0:a * b].rearrange("p (a b) -> p a b", a=a)
        key = "wslot%d" % i
        P.add(queue, lambda e: e.dma_start(out=view, in_=src_ap), r=["wscr"], w=[key], dma=key)
        return view, key

    acc_ctr = [0]

    def next_acc():
        i = acc_ctr[0] % 2
        acc_ctr[0] += 1
        return ps[i], "ps%d" % i

    ev_ctr = [0]

    P.add('sp', lambda e: e.dma_start(out=cst[:], in_=cst_d[:, 0:NCONST]), w=["cst"], dma="cst")
    P.add('sp', lambda e: e.dma_start(out=ft[0][:, 0:512], in_=cst_d[:, C_AMASK:C_AMASK + 512]), w=["ft0"], dma="cst2")
    CP(idb[:], idf, r=["cst"], w=["idb"])
    dve(lambda e: e.memset(onesb[:], 1.0), (), ["onesb"])
    CP(amaskb[:], ft[0][:, 0:512], r=["ft0"], w=["amaskb"])
    pool(lambda e: e.memset(Cst[:], 0.0), (), ["Cst"])
    pool(lambda e: e.memset(qk_tail[:], 0.0), (), ["qk_tail"])
    pool(lambda e: e.memset(ka_tail[:], 0.0), (), ["ka_tail"])
    pool(lambda e: e.memset(va_tail[:], 0.0), (), ["va_tail"])
    pool(lambda e: e.memset(ffn_tail[:], 0.0), (), ["ffn_tail"])
    pool(lambda e: e.memset(gstate[:], 0.0), (), ["gstate"])
    pool(lambda e: e.memset(gprev[:], 0.0), (), ["gprev"])
    pool(lambda e: e.memset(rows[:], 0.0), (), ["rows"])
    pool(lambda e: e.memset(vm[:], 1.0), (), ["vm"])
    pool(lambda e: e.memset(va[:], 1.0), (), ["va"])

    def cast(dst, src, l):
        P.add('pool', lambda e: e.dma_start(out=dst, in_=src), r=(), w=(), dma=("cast", l),
              w_nodep=["wscr"])

    def colview(w2d, c0, n):
        return w2d[:, c0:c0 + n].rearrange("(kc p) c -> p kc c", p=128)

    for l in range(L if STOP >= 2 else 0):
        w = win_d[l]
        srcs = [0, 256, 512, 768, 1024, 1280, 1536, 1792, 2048, 2304, 2560, 2816,
                3080, 3336, 3592, 3848]
        for b, c0 in enumerate(srcs):
            cast(WI[l, b], colview(w, c0, 256), l)
        cast(WI[l, 16][:, :, 0:64], colview(w, 4104, 64), l)
        cast(WI[l, 16][:, :, 64:128], colview(w, 4104, 64), l)
        cast(WI[l, 16][:, :, 128:192], colview(w, 4168, 64), l)
        cast(WI[l, 16][:, :, 192:256], colview(w, 4168, 64), l)
        cast(WI[l, 17][:, :, 0:128], colview(w, 4232, 128), l)
        cast(WI[l, 17][:, :, 128:136], colview(w, 3072, 8), l)
        for b in range(8):
            cast(WO[l, b], colview(wout_d[l], b * 256, 256), l)
        for b in range(44):
            cast(WU[l, b], colview(wup_d[l], b * 256, 256), l)
        for g, (k0, k1) in enumerate(FFN_GROUPS):
            nk = k1 - k0
            for cb in range(8):
                src = wdn_d[l][k0 * 128:k1 * 128, cb * 256:(cb + 1) * 256].rearrange(
                    "(kc p) c -> p kc c", p=128)
                cast(WD[l, g, cb][:, 0:nk, :], src, l)

    gl32 = sb("gl32", [128, 32], F32)

    def load_T(rows_list, dst_ap, n):
        r0 = 0
        for j, (src, k) in enumerate(rows_list):
            fn = (lambda s, a, b: (lambda e: e.dma_start(out=pstage[a:b, :], in_=s)))(src, r0, r0 + k)
            if j == 0:
                P.add('sp', fn, r=(), w=["pstage"], dma="pstage")
            else:
                P.add('sp', fn, r=(), w=(), dma="pstage", w_nodep=["pstage"])
            r0 += k
        assert r0 == n
        tr(ps[2][:, 0:n], pstage[0:n, :], idf[0:n, 0:n], r=["pstage", "cst"], w=["ps2"])
        CP(dst_ap, ps[2][:, 0:n], r=["ps2"], w=["params"])

    if STOP >= 3:
      load_T([(c_d, 16), (fg_d, 16)], gl32[:, 0:32], 32)
    if STOP >= 3:
      A_(cact[:], gl32[:, 0:16], AF.Silu, r=["params"], w=["cact"])
      CP(fgc[:], gl32[:, 16:32], r=["params"], w=["fgc"])
    def adaln(l):
        for j in range(96):
            i = slot_ctr[0] % NSLOT
            slot_ctr[0] += 1
            view = wslot[i][:].bitcast(F32)[:, 0:KC * 128].rearrange("p (a b) -> p a b", a=KC)
            key = "wslot%d" % i
            src = adaw_d[l][:, j * 128:(j + 1) * 128].rearrange("(kc p) c -> p kc c", p=128)
            P.add('sp', (lambda v, s: (lambda e: e.dma_start(out=v, in_=s)))(view, src), r=(), w=[key], dma=key)
            for kc in range(KC):
                mm(ps[3][:, j:j + 1], view[:, kc, :], cact[:, kc:kc + 1], kc == 0, kc == KC - 1,
                   r=[key, "cact"], w=["ps3"])
        TT(modp[:, l, :], ps[3][:, 0:96], PA[:, 0:96], ALU.add, r=["ps3", "params"], w=["modp%d" % l])
        STT(modp[:, l, 16:32], modp[:, l, 16:32], 1.0, PA[:, 96:112], ALU.add, ALU.mult,
            r=["modp%d" % l, "params"], w=["modp%d" % l])
        STT(modp[:, l, 64:80], modp[:, l, 64:80], 1.0, PA[:, 112:128], ALU.add, ALU.mult,
            r=["modp%d" % l, "params"], w=["modp%d" % l])


    for l in range(L if STOP >= 3 else 0):
        load_T([(adab_d[l], 96), (nmix_d[l], 16), (nffn_d[l], 16)], PA[:, :], 128)
        load_T([(cffn_d[l][0:128, :], 128)], PB1[:, l, :], 128)
        load_T([(cffn_d[l][128:132, :], 4), (cqk_d[l], 32), (hg_d[l], 8)], PB2[:, l, :], 44)
        P.add('sp', (lambda ll: (lambda e: e.dma_start(out=esink[:, ll, :], in_=sink_d[ll].broadcast_to([128, 16]))))(l),
              r=(), w=["esink%d" % l], dma="esink%d" % l)
        A_(esink[:, l, :], esink[:, l, :], AF.Exp, r=["esink%d" % l], w=["esink%d" % l])
        P.add('sp', (lambda ll: (lambda e: e.dma_start(out=gbias[:, ll, 0:1], in_=bg_d[ll][0:4, :])))(l),
              r=(), w=["gbias%d" % l], dma="gbias%d" % l)
        P.add('sp', (lambda ll: (lambda e: e.dma_start(out=gbias[:, ll, 1:2], in_=bg_d[ll][4:8, :])))(l),
              r=(), w=(), dma="gbias%d" % l, w_nodep=["gbias%d" % l])
        TS(gbias[:, l, :], gbias[:, l, :], 1.0 / 15.0, None, ALU.mult, r=["gbias%d" % l], w=["gbias%d" % l])
        if STOP >= 4:
            adaln(l)

    def cffn_col(l, k, j):
        idx = k * NFC + j
        return PB1[:, l, idx:idx + 1] if idx < 128 else PB2[:, l, idx - 128:idx - 127]

    def cqk_col(l, k, c):
        return PB2[:, l, 4 + k * 8 + c:5 + k * 8 + c]

    def hg_col(l, c):
        return PB2[:, l, 36 + c:37 + c]

    def rms_rstd(tag):
        for kc in range(KC):
            s = sqb[kc % 2]
            A_(s[:], xT[:, kc, :], AF.Square, r=["x%d" % kc], w=["sqb%d" % (kc % 2)])
            mm(ps[2][:], onesb[:], s[:], kc == 0, kc == KC - 1, r=["sqb%d" % (kc % 2), "onesb"], w=["ps2"])
        A_(rs_tmp[:], ps[2][:], AF.Ln, bias=EPS, scale=1.0 / D, r=["ps2"], w=["rs_tmp"])
        A_(rstd[:], rs_tmp[:], AF.Exp, scale=-0.5, r=["rs_tmp"], w=["rstd"])

    def norm_mod(l, gs_off, sh_off):
        rms_rstd("n")
        for kc in range(KC):
            tn = tmpn[kc % 2]
            STT(tn[:], xT[:, kc, :], modp[:, l, gs_off + kc:gs_off + kc + 1], rstd[:], ALU.mult, ALU.mult,
                r=["x%d" % kc, "modp%d" % l, "rstd"], w=["tmpn%d" % (kc % 2)])
            A_(hT[:, kc, :], tn[:], AF.Identity, bias=modp[:, l, sh_off + kc:sh_off + kc + 1],
               r=["tmpn%d" % (kc % 2), "modp%d" % l], w=["h%d" % kc])

    hkeys = ["h%d" % kc for kc in range(KC)]

    def proj_fm(view, key, c0, ncols, out_ps, pskeys, rhs_cols=None):
        for kc in range(KC):
            mm(out_ps, view[:, kc, c0:c0 + ncols], hT[:, kc, :], kc == 0, kc == KC - 1,
               r=[key, "h%d" % kc], w=pskeys)

    def load_x_tile(t):
        for blk in range(NBLK):
            r0 = t * T + blk * 128
            P.add('pool', (lambda a: (lambda e: e.dma_start(out=xstage[:], in_=x_d[a:a + 128, :])))(r0),
                  r=(), w=["xstage"], dma="xstage")
            for q4 in range(4):
                bank = ps[4 + (q4 % 2)]
                bkey = ["ps%da" % (4 + q4 % 2), "ps%db" % (4 + q4 % 2)]
                for j in range(4):
                    kc = q4 * 4 + j
                    tr(bank[:, j * 128:(j + 1) * 128], xstage[:, kc * 128:(kc + 1) * 128], idf,
                       r=["xstage", "cst"], w=bkey)
                dst = xT[:, q4 * 4:q4 * 4 + 4, blk * 128:(blk + 1) * 128]
                src = bank[:].rearrange("p (a b) -> p a b", a=4)
                if q4 % 2 == 0:
                    CP(dst, src, r=bkey, w=["x%d" % (q4 * 4 + j) for j in range(4)])
                else:
                    CP(dst, src, r=bkey, w=["x%d" % (q4 * 4 + j) for j in range(4)], eng='act')

    def rope_tables(t):
        P.add('sp', (lambda a: (lambda e: e.dma_start(out=posi[:], in_=pos_d[:, a:a + T].broadcast_to([128, T]))))(t * T),
              r=(), w=["posi"], dma="posi")
        CP(ang[:], posi[:], r=["posi"], w=["ang"])
        TS(ang[:], ang[:], invf, None, ALU.mult, r=["ang", "cst"], w=["ang"])
        C1 = 6.28125
        C2 = TWO_PI - C1
        yv, kf, tv = ft[0][:, 0:T], ft[2][:, 0:T], ft[3][:, 0:T]
        ki = ft[1][:, 0:T].bitcast(I32)
        for tab, ph, key in ((cosT, 0.25, "cosT"), (sinT, 0.0, "sinT")):
            TS(yv, ang[:], 1.0 / TWO_PI, ph, ALU.mult, ALU.add, r=["ang"], w=["ft0"])
            CP(ki, yv, r=["ft0"], w=["ft1"])
            CP(kf, ki, r=["ft1"], w=["ft2"])
            STT(tv, kf, -C1, ang[:], ALU.mult, ALU.add, r=["ft2", "ang"], w=["ft3"])
            STT(tv, kf, -C2, tv, ALU.mult, ALU.add, r=["ft2", "ft3"], w=["ft3"])
            TS(tv, tv, ph * TWO_PI, math.pi, ALU.add, ALU.min, r=["ft3"], w=["ft3"])
            TS(tv, tv, -math.pi, None, ALU.max, r=["ft3"], w=["ft3"])
            A_(tab[:], tv, AF.Sin, r=["ft3"], w=[key])

    def final_out(t):
        rms_rstd("f")
        for blk in range(NBLK):
            for q4 in range(4):
                bank = ps[4 + (q4 % 2)]
                bkey = ["ps%da" % (4 + q4 % 2), "ps%db" % (4 + q4 % 2)]
                for j in range(4):
                    kc = q4 * 4 + j
                    tn = tmpn[kc % 2]
                    STT(tn[:, 0:128], xT[:, kc, blk * 128:(blk + 1) * 128], fgc[:, kc:kc + 1],
                        rstd[:, blk * 128:(blk + 1) * 128], ALU.mult, ALU.mult,
                        r=["x%d" % kc, "fgc", "rstd"], w=["tmpn%d" % (kc % 2)])
                    tr(bank[:, j * 128:(j + 1) * 128], tn[:, 0:128], idf, r=["tmpn%d" % (kc % 2), "cst"], w=bkey)
                CP(xstage[:, q4 * 512:(q4 + 1) * 512], bank[:], r=bkey, w=["xstage"],
                   eng=('dve' if q4 % 2 == 0 else 'act'))
            r0 = t * T + blk * 128
            P.add('pool', (lambda a: (lambda e: e.dma_start(out=out_d[a:a + 128, :], in_=xstage[:])))(r0),
                  r=["xstage"], w=["outd"], dma="outd")

    def evac_alt():
        ev_ctr[0] += 1
        return 'act' if ev_ctr[0] % 2 == 0 else 'dve'

    def mixer(l, t):
        norm_mod(l, 16, 0)
        WIl = WI[l]
        for b in range(4):
            view, key = wload(WIl[b], (KC, 256))
            for cc in range(2):
                c = b * 2 + cc
                acc, akey = next_acc()
                proj_fm(view, key, cc * 128, 128, acc[:], [akey])
                qt = qktmp[c % 2]
                qk_ = "qktmp%d" % (c % 2)
                CP(qt[:, 3:T + 3], acc[:], r=[akey], w=[qk_], eng='act')
                CP(qt[:, 0:3], qk_tail[:, l, c, :], r=["qk_tail"], w=[qk_ + "t"], eng='pool')
                ca = cacc[c % 2]
                ck = "cacc%d" % (c % 2)
                TS(ca[:], qt[:, 0:T], cqk_col(l, 0, c), None, ALU.mult, r=[qk_, qk_ + "t", "params"], w=[ck])
                for k in range(1, 4):
                    STT(ca[:], qt[:, k:k + T], cqk_col(l, k, c), ca[:], ALU.mult, ALU.add,
                        r=[qk_, qk_ + "t", "params", ck], w=[ck])
                CP(qk_tail[:, l, c, :], qt[:, T:T + 3], r=[qk_], w=["qk_tail"], eng='pool')
                A_(qkT[:, c, :], ca[:], AF.Silu, r=[ck], w=["qkT%d" % c])
        if STOP < 7.1:
            return
        for h in range(4):
            view, key = wload(WIl[4 + h], (KC, 256))
            for blk in range(NBLK):
                acc, akey = next_acc()
                for kc in range(KC):
                    mm(acc[:, 0:256], hT[:, kc, blk * 128:(blk + 1) * 128], view[:, kc, :], kc == 0, kc == KC - 1,
                       r=[key, "h%d" % kc], w=[akey])
                CP(vm[:, blk, h, 0:256], acc[:, 0:256], r=[akey], w=["vm"], eng=evac_alt())
        if STOP < 7.2:
            return
        for b in range(4):
            view, key = wload(WIl[8 + b], (KC, 256))
            for cc in range(2):
                c = b * 2 + cc
                acc, akey = next_acc()
                proj_fm(view, key, cc * 128, 128, acc[:], [akey])
                A_(oT[:, c, :], acc[:], AF.Sigmoid, r=[akey], w=["oT%d" % c])
        if STOP < 7.3:
            return
        def rope_chunk(acc, akey, dst, dkeys, idx):
            tq = tmpq[idx % 2]
            tk = "tmpq%d" % (idx % 2)
            CP(tq[:], acc[:], r=[akey], w=[tk], eng='act')
            pb = ps[2 + (idx % 2)]
            pk = ["ps2"] if idx % 2 == 0 else ["ps3"]
            mm(pb[:], perm, tq[:], True, True, r=[tk, "cst"], w=pk)
            TT(rt1[:], pb[:], sinT[:], ALU.mult, r=pk + ["sinT"], w=["rt1"])
            TT(rt2[:], tq[:], cosT[:], ALU.mult, r=[tk, "cosT"], w=["rt2"], eng='pool')
            TT(dst, rt1[:], rt2[:], ALU.add, r=["rt1", "rt2"], w=dkeys)

        for b in range(4):
            view, key = wload(WIl[12 + b], (KC, 256))
            for cc in range(2):
                c = b * 2 + cc
                acc, akey = next_acc()
                proj_fm(view, key, cc * 128, 128, acc[:], [akey])
                rope_chunk(acc, akey, qaT[:, c, :], ["qaT%d" % c], c)
        view, key = wload(WIl[16], (KC, 256))
        CP(kaT[:, :, 0:128], ka_tail[:, l, :, :], r=["ka_tail"], w=["kaT_t"], eng='pool')
        for g in range(2):
            acc, akey = next_acc()
            proj_fm(view, key, g * 128, 128, acc[:], [akey])
            rope_chunk(acc, akey, kaT[:, g, 128:T + 128], ["kaT%d" % g], g)
        if STOP < 7.4:
            return
        view, key = wload(WIl[17][:, :, 0:136], (KC, 136))
        CP(va[:, 0, :, :], va_tail[:, l, :, :], r=["va_tail"], w=["va_t"], eng='pool')
        for blk in range(NBLK):
            acc, akey = next_acc()
            for kc in range(KC):
                mm(acc[:, 0:128], hT[:, kc, blk * 128:(blk + 1) * 128], view[:, kc, 0:128], kc == 0, kc == KC - 1,
                   r=[key, "h%d" % kc], w=[akey])
            CP(va[:, 1 + blk, :, 0:64], acc[:, 0:128].rearrange("p (g d) -> p g d", g=2), r=[akey], w=["va"],
               eng=evac_alt())
        if STOP < 7.5:
            return
        acc_i, ki = next_acc()
        for kc in range(KC):
            mm(acc_i[0:4, :], view[:, kc, 128:132], hT[:, kc, :], kc == 0, kc == KC - 1, r=[key, "h%d" % kc], w=[ki])
        A_(g_ti[:], acc_i[0:4, :], AF.Tanh, bias=gbias[:, l, 0:1], scale=1.0 / 15.0, r=[ki, "gbias%d" % l], w=["g_ti"])
        acc_f, kf = next_acc()
        for kc in range(KC):
            mm(acc_f[0:4, :], view[:, kc, 132:136], hT[:, kc, :], kc == 0, kc == KC - 1, r=[key, "h%d" % kc], w=[kf])
        A_(g_tf[:], acc_f[0:4, :], AF.Tanh, bias=gbias[:, l, 1:2], scale=1.0 / 15.0, r=[kf, "gbias%d" % l], w=["g_tf"])
        if STOP < 7.51:
            return
        A_(g_e[:], g_tf[:], AF.Exp, scale=-15.0, r=["g_tf"], w=["g_e"])
        A_(g_e[:], g_e[:], AF.Ln, bias=1.0, r=["g_e"], w=["g_e"])
        TS(g_lfh[:], g_e[:], -0.5, None, ALU.mult, r=["g_e"], w=["g_lfh"])
        if STOP < 7.52:
            return
        dve(lambda e: e.tensor_tensor_scan(g_B[:], g_lfh[:], g_lfh[:], gstate[:, l, 0:1], ALU.add, ALU.add),
            ["g_lfh", "gstate"], ["g_B"])
        STT(rows[0:4, :], g_ti[:], 15.0, g_B[:], ALU.mult, ALU.subtract, r=["g_ti", "g_B"], w=["rowsA"])
        dve(lambda e: e.tensor_tensor_scan(g_G[:], rows[0:4, :], rows[0:4, :], gstate[:, l, 1:2], ALU.max, ALU.max),
            ["rowsA", "gstate"], ["g_G"])
        if STOP < 7.53:
            return
        CP(rows[32:36, :], g_G[:], r=["g_G"], w=["rowsG"])
        STT(rows[64:68, :], g_B[:], -1.0, g_G[:], ALU.mult, ALU.subtract, r=["g_B", "g_G"], w=["rowsM"])
        A_(rows[64:68, :], rows[64:68, :], getattr(AF, os.environ.get("KFUNC", "Exp")), r=["rowsM"], w=["rowsM"])
        CP(gstate[:, l, 0:1], g_B[:, T - 1:T], r=["g_B"], w=["gstate"])
        CP(gstate[:, l, 1:2], g_G[:, T - 1:T], r=["g_G"], w=["gstate"])
        if STOP < 7.54:
            return
        for blk in range(NBLK):
            tr(ps[2][:, blk * 96:(blk + 1) * 96], rows[0:96, blk * 128:(blk + 1) * 128], idf[0:96, 0:96],
               r=["rowsA", "rowsG", "rowsM", "rows", "cst"], w=["ps2"])
        if STOP < 7.55:
            return
        CP(gcols[:], ps[2][:, 0:NBLK * 96].rearrange("p (a b) -> p a b", a=NBLK), r=["ps2"], w=["gcols"])
        if os.environ.get('KDUMP') == 'gcols':
            P.add('sp', lambda e: e.dma_start(out=out_d[0:128, 0:384], in_=gcols[:].rearrange('p a b -> p (a b)')), r=['gcols'], w=['outd'], dma='outd')
            P.add('sp', lambda e: e.dma_start(out=out_d[128:132, 0:512], in_=g_G[:]), r=['g_G'], w=['outd'], dma='outd')
            P.add('sp', lambda e: e.dma_start(out=out_d[132:136, 0:512], in_=g_B), r=['g_B'], w=['outd'], dma='outd')
            P.add('sp', lambda e: e.dma_start(out=out_d[136:140, 0:512], in_=g_ti), r=['g_ti'], w=['outd'], dma='outd')
            P.add('sp', lambda e: e.dma_start(out=out_d[140:144, 0:512], in_=g_lfh), r=['g_lfh'], w=['outd'], dma='outd')
        if STOP < 7.56:
            return

        if STOP < 7.6:
            return
        psG, psS, psI, psC, psU = ps[3], ps[4], ps[5], ps[6], ps[7]
        psKT = ps[4][:].bitcast(BF16)
        psYT = ps[7][:].bitcast(BF16)
        for h in range(4):
            mm(psG[:], cst[0:4, C_SEL + h * 128:C_SEL + (h + 1) * 128], g_G[:], True, False, r=["cst", "g_G"], w=["ps3"])
            mm(psG[:], idf, mmask, False, True, r=["cst"], w=["ps3"])
            CP(gend[:, h, 0:1], gprev[:, l, h:h + 1], r=["gprev"], w=["gend%d" % h], eng='pool')
            CP(gend[:, h, 1:5], psG[:].rearrange("p (a b) -> p a b", a=4)[:, :, 127], r=["ps3"], w=["gend%d" % h])
            CP(gprev[:, l, h:h + 1], gend[:, h, 4:5], r=["gend%d" % h], w=["gprev"], eng='pool')
            CP(Cb[:, h, :], Cst[:, l, h, :], r=["Cst%d" % h, "Cst"], w=["Cb%d" % h], eng='act')
            for c in range(NBLK if STOP >= 7.61 else 0):
                i2 = (h * NBLK + c) % 2
                cs = slice(c * 128, (c + 1) * 128)
                qTc = qkT[:, h, cs]
                kTc = qkT[:, 4 + h, cs]
                mc = mcol[i2]
                mk = "mcol%d" % i2
                mm(psS[:, 0:128], kTc, qTc, True, True, r=["qkT%d" % h, "qkT%d" % (4 + h)], w=["ps4a"])
                if STOP < 7.611:
                    continue
                A_(wexp[i2][:], psG[:, cs], AF.Exp, bias=gcols[:, c, h:h + 1], scale=-1.0,
                   r=["ps3", "gcols"], w=["wexp%d" % i2])
                if STOP < 7.612:
                    continue
                STT(stw[i2][:], psS[:, 0:128], QS, wexp[i2][:], ALU.mult, ALU.mult,
                    r=["ps4a", "wexp%d" % i2], w=["stw%d" % i2])
                if STOP < 7.613:
                    continue
                mm(psI[:, 0:260], stw[i2][:], vm[:, c, h, :], True, True, r=["stw%d" % i2, "vm"], w=["ps5"])
                if STOP < 7.614:
                    continue
                mm(psC[:, 0:260], qTc, Cb[:, h, :], True, True, r=["qkT%d" % h, "Cb%d" % h], w=["ps6"])
                if STOP < 7.62:
                    continue
                A_(mc[:, 0:1], gcols[:, c, 32 + h:33 + h], AF.Exp, bias=gend[:, h, c:c + 1], scale=-1.0,
                   r=["gcols", "gend%d" % h], w=[mk])
                TS(mtmp[i2][:], psC[:, 0:260], mc[:, 0:1], QS, ALU.mult, ALU.mult, r=["ps6", mk], w=["mtmp%d" % i2])
                TT(hun[i2][:], mtmp[i2][:], psI[:, 0:260], ALU.add, r=["mtmp%d" % i2, "ps5"], w=["hun%d" % i2])
                if STOP < 7.63:
                    continue
                STT(mc[:, 11:12], hun[i2][:, 256:257], -1.0, hun[i2][:, 256:257], ALU.mult, ALU.max,
                    r=["hun%d" % i2], w=[mk])
                TT(mc[:, 1:2], mc[:, 11:12], gcols[:, c, 64 + h:65 + h], ALU.max, r=[mk, "gcols"], w=[mk])
                dve(lambda e, a=mc[:, 2:3], b=mc[:, 1:2]: e.reciprocal(a, b), [mk], [mk])
                TT(sqj[:], hun[i2][:, 0:256], hun[i2][:, 0:256], ALU.mult, r=["hun%d" % i2], w=["sqj"])
                dve(lambda e, o=mc[:, 3:4], a=sqj[:]: e.tensor_reduce(o, a, AX.X, ALU.add), ["sqj"], [mk])
                STT(mc[:, 4:5], mc[:, 2:3], 1.0 / 256.0, mc[:, 2:3], ALU.mult, ALU.mult, r=[mk], w=[mk])
                TT(mc[:, 5:6], mc[:, 4:5], mc[:, 3:4], ALU.mult, r=[mk], w=[mk])
                A_(mc[:, 6:7], mc[:, 5:6], AF.Ln, bias=EPS, r=[mk], w=[mk])
                A_(mc[:, 7:8], mc[:, 6:7], AF.Exp, scale=-0.5, r=[mk], w=[mk])
                TT(mc[:, 8:9], mc[:, 7:8], mc[:, 2:3], ALU.mult, r=[mk], w=[mk])
                TS(ytm[i2][:], hun[i2][:, 0:256], mc[:, 8:9], None, ALU.mult, r=["hun%d" % i2, mk], w=["ytm%d" % i2])
                if STOP < 7.64:
                    continue
                for j in range(2):
                    fc = h * 2 + j
                    tr(psYT[:, 768 + j * 128:768 + (j + 1) * 128], ytm[i2][:, j * 128:(j + 1) * 128], idb[:],
                       r=["ytm%d" % i2, "idb"], w=["ps7b"])
                    STT(hT[:, fc, cs], psYT[:, 768 + j * 128:768 + (j + 1) * 128], hg_col(l, fc), oT[:, fc, cs],
                        ALU.mult, ALU.mult, r=["ps7b", "params", "oT%d" % fc], w=["h%d" % fc])
                if STOP < 7.65:
                    continue
                A_(mc[:, 9:10], gend[:, h, c + 1:c + 2], AF.Exp, bias=gcols[:, c, h:h + 1], scale=-1.0,
                   r=["gend%d" % h, "gcols"], w=[mk])
                A_(mc[:, 10:11], gend[:, h, c + 1:c + 2], AF.Exp, bias=gend[:, h, c:c + 1], scale=-1.0,
                   r=["gend%d" % h], w=[mk])
                A_(vw[i2][:], vm[:, c, h, :], AF.Identity, scale=mc[:, 9:10], r=["vm", mk], w=["vw%d" % i2])
                tr(psKT[:, 512:640], kTc, idb[:], r=["qkT%d" % (4 + h), "idb"], w=["ps4b"])
                CP(ktok[i2][:], psKT[:, 512:640], r=["ps4b"], w=["ktok%d" % i2], eng='pool' if False else 'dve')
                mm(psU[:, 0:260], ktok[i2][:], vw[i2][:], True, True, r=["ktok%d" % i2, "vw%d" % i2], w=["ps7a"])
                STT(Cst[:, l, h, :], Cst[:, l, h, :], mc[:, 10:11], psU[:, 0:260], ALU.mult, ALU.add,
                    r=["Cst%d" % h, "Cst", mk, "ps7a"], w=["Cst%d" % h])
                CP(Cb[:, h, :], Cst[:, l, h, :], r=["Cst%d" % h], w=["Cb%d" % h], eng='act')

        if STOP < 7.7:
            return
        for n in range(NBLK):
            qs = slice(n * 128, (n + 1) * 128)
            for hq in range(4):
                pv = ps[5 + (hq % 2)]
                pvk = ["ps5"] if hq % 2 == 0 else ["ps6"]
                scs = [ps[3], ps[4]]
                scks = [["ps3"], ["ps4"]]
                for hh in range(2):
                    mm(scs[hh][:], idb[:], amaskb[:], True, False, r=["idb", "amaskb"], w=scks[hh])
                for pr in range(2):
                    ci = hq * 2 + pr
                    g = ci // 4
                    for hh in range(2):
                        pbase = hh * 64
                        for kb in range(2):
                            last = (pr == 1 and kb == 1)
                            mm(scs[hh][:, pr * 256 + kb * 128:pr * 256 + (kb + 1) * 128],
                               kaT[pbase:pbase + 64, g, (n + kb) * 128:(n + kb + 1) * 128],
                               qaT[pbase:pbase + 64, ci, qs], False, last,
                               r=["kaT%d" % g, "kaT_t", "qaT%d" % ci], w=scks[hh])
                for hh in range(2):
                    A_(pT[hh][:], scs[hh][:], AF.Exp, scale=0.125, r=scks[hh], w=["pT%d" % hh])
                for pr in range(2):
                    g = (hq * 2 + pr) // 4
                    for hh in range(2):
                        col = (pr * 2 + hh) * 68
                        for kb in range(2):
                            mm(pv[:, col:col + 68], pT[hh][:, pr * 256 + kb * 128:pr * 256 + (kb + 1) * 128],
                               va[:, n + kb, g, :], kb == 0, kb == 1, r=["pT%d" % hh, "va", "va_t"], w=pvk)
                ac = acol[hq % 2]
                ak = "acol%d" % (hq % 2)
                pv3 = pv[:, 0:272].rearrange("p (a b) -> p a b", a=4)
                TT(ac[:, 0:4], pv3[:, :, 64], esink[:, l, hq * 4:hq * 4 + 4], ALU.add, r=pvk + ["esink%d" % l], w=[ak])
                dve(lambda e, a=ac[:, 4:8], b=ac[:, 0:4]: e.reciprocal(a, b), [ak], [ak])
                TT(yta[:, hq * 256:(hq + 1) * 256].rearrange("p (a b) -> p a b", a=4), pv3[:, :, 0:64],
                   ac[:, 4:8].unsqueeze(2).broadcast_to([128, 4, 64]), ALU.mult, r=pvk + [ak], w=["yta"])
            psT7 = ps[7][:].bitcast(BF16)
            for j in range(8):
                half = j % 2
                tr(psT7[:, half * 512 + (j // 2 % 2) * 128: half * 512 + (j // 2 % 2) * 128 + 128],
                   yta[:, j * 128:(j + 1) * 128], idb[:], r=["yta", "idb"], w=["ps7a", "ps7b"])
                CP(hT[:, 8 + j, qs], psT7[:, half * 512 + (j // 2 % 2) * 128: half * 512 + (j // 2 % 2) * 128 + 128],
                   r=["ps7a", "ps7b"], w=["h%d" % (8 + j)], eng=evac_alt())
        CP(ka_tail[:, l, :, :], kaT[:, :, T:T + 128], r=["kaT0", "kaT1"], w=["ka_tail"], eng='pool')
        CP(va_tail[:, l, :, :], va[:, NBLK, :, :], r=["va"], w=["va_tail"], eng='pool')

        if STOP < 7.8:
            return
        for b in range(8):
            view, key = wload(WO[l, b], (KC, 256))
            for cc in range(2):
                oc = b * 2 + cc
                acc, akey = next_acc()
                proj_fm(view, key, cc * 128, 128, acc[:], [akey])
                STT(xT[:, oc, :], acc[:], modp[:, l, 32 + oc:33 + oc], xT[:, oc, :], ALU.mult, ALU.add,
                    r=[akey, "modp%d" % l, "x%d" % oc], w=["x%d" % oc])

    def ffn(l, t):
        norm_mod(l, 64, 48)
        bank_i = [0]

        def nb():
            i = bank_i[0] % 8
            bank_i[0] += 1
            keys = ["ps%d" % i] if i not in (4, 7) else ["ps%da" % i, "ps%db" % i]
            return ps[i], keys

        for g, (k0, k1) in enumerate(FFN_GROUPS):
            nk = k1 - k0
            for bb in range(k0 // 2, k1 // 2):
                gv, gk = wload(WU[l, bb], (KC, 256))
                uv, uk = wload(WU[l, 22 + bb], (KC, 256))
                for cc in range(2):
                    j = bb * 2 + cc
                    i2 = j % 2
                    pg, pgk = nb()
                    proj_fm(gv, gk, cc * 128, 128, pg[:], pgk)
                    pu, puk = nb()
                    proj_fm(uv, uk, cc * 128, 128, pu[:], puk)
                    if STOP < 8.2:
                        continue
                    i4 = j % 4
                    gb = gbuf4[i4]
                    gbk = "gbuf%d" % i4
                    fa = ffacc4[i4]
                    fk = "ffacc%d" % i4
                    CP(gb[:, 2:T + 2], pg[:], r=pgk, w=[gbk], eng='act')
                    A_(fa[:], pg[:], AF.Identity, scale=cffn_col(l, 2, j), r=pgk + ["params"], w=[fk])
                    CP(gb[:, 0:2], ffn_tail[:, l, j, :], r=["ffn_tail"], w=[gbk + "t"], eng='pool')
                    STT(fa[:], gb[:, 1:T + 1], cffn_col(l, 1, j), fa[:], ALU.mult, ALU.add,
                        r=[gbk, gbk + "t", "params", fk], w=[fk])
                    STT(fa[:], gb[:, 0:T], cffn_col(l, 0, j), fa[:], ALU.mult, ALU.add,
                        r=[gbk, gbk + "t", "params", fk], w=[fk])
                    CP(ffn_tail[:, l, j, :], gb[:, T:T + 2], r=[gbk], w=["ffn_tail"], eng='pool')
                    A_(fa[:], fa[:], AF.Silu, r=[fk], w=[fk])
                    TT(actb[:, j - k0, :], fa[:], pu[:], ALU.mult, r=[fk] + puk, w=["actb%d" % (j - k0)])
            for cb in range(8 if STOP >= 8.5 else 0):
                dv, dk = wload(WD[l, g, cb][:, 0:nk, :], (nk, 256))
                for cc in range(2):
                    oc = cb * 2 + cc
                    po, pok = nb()
                    for k in range(nk):
                        mm(po[:], dv[:, k, cc * 128:(cc + 1) * 128], actb[:, k, :], k == 0, k == nk - 1,
                           r=[dk, "actb%d" % k], w=pok)
                    STT(xT[:, oc, :], po[:], modp[:, l, 80 + oc:81 + oc], xT[:, oc, :], ALU.mult, ALU.add,
                        r=pok + ["modp%d" % l, "x%d" % oc], w=["x%d" % oc])

    for t in range(NT if STOP >= 5 else 0):
        load_x_tile(t)
        if STOP >= 6:
            rope_tables(t)
        for l in range(L if STOP >= 7 else 0):
            mixer(l, t)
            if STOP >= 8:
                ffn(l, t)
        if not os.environ.get('KDUMP'):
            final_out(t)
    P.add('sp', None, r=["outd"], w=())
    P.add('pool', None, r=["outd"], w=())
    P.emit(nc, es)
    es.close()
    return nc


def make_in_map(b, inp, L, S, consts):
    f = np.float32
    return {
        "x": np.ascontiguousarray(inp["x"][b, :S], dtype=f),
        "c": np.ascontiguousarray(inp["c"][b], dtype=f).reshape(KC, 128),
        "positions": np.ascontiguousarray(inp["positions"][b, :S], dtype=np.int32).reshape(1, S),
        "ada_w": np.ascontiguousarray(inp["ada_w"][:L], dtype=f),
        "ada_b": np.ascontiguousarray(inp["ada_b"][:L], dtype=f).reshape(L, 96, 128),
        "norm_mix_g": np.ascontiguousarray(inp["norm_mix_g"][:L], dtype=f).reshape(L, KC, 128),
        "w_in": np.ascontiguousarray(inp["w_in"][:L], dtype=f),
        "b_gates": np.ascontiguousarray(inp["b_gates"][:L], dtype=f).reshape(L, 8, 1),
        "conv_qk": np.ascontiguousarray(inp["conv_qk"][:L], dtype=f).reshape(L, 32, 128),
        "mlstm_head_g": np.ascontiguousarray(inp["mlstm_head_g"][:L], dtype=f).reshape(L, 8, 128),
        "sinks": np.ascontiguousarray(inp["sinks"][:L], dtype=f).reshape(L, 1, 16),
        "w_out": np.ascontiguousarray(inp["w_out"][:L], dtype=f),
        "norm_ffn_g": np.ascontiguousarray(inp["norm_ffn_g"][:L], dtype=f).reshape(L, KC, 128),
        "w_up": np.ascontiguousarray(inp["w_up"][:L], dtype=f),
        "conv_ffn": np.ascontiguousarray(inp["conv_ffn"][:L], dtype=f).reshape(L, 132, 128),
        "w_down": np.ascontiguousarray(inp["w_down"][:L], dtype=f),
        "final_g": np.ascontiguousarray(inp["final_g"], dtype=f).reshape(KC, 128),
        "consts": consts,
    }


_NC_CACHE = {}


def run_model(inp, L, S, batch_ids):
    key = (L, S)
    if key not in _NC_CACHE:
        _NC_CACHE[key] = build_program(L, S // T)
    nc = _NC_CACHE[key]
    consts = make_consts()
    in_maps = [make_in_map(b, inp, L, S, consts) for b in batch_ids]
    res = run_bass_kernel_spmd(nc, in_maps, core_ids=list(range(len(batch_ids))))
    return np.stack([r["out"] for r in res.results], axis=0)


def kernel(**inputs):
    inp = {k: np.asarray(v) for k, v in inputs.items()}
    B, S, _ = inp["x"].shape
    L = inp["ada_w"].shape[0]
    out = run_model(inp, L, S, list(range(B)))
    return out.astype(np.float32)
```

```python
import math
import os
from contextlib import ExitStack

import numpy as np
import concourse.bass as bass
import concourse.mybir as mybir
from concourse.bass_utils import run_bass_kernel_spmd

F32 = mybir.dt.float32
BF16 = mybir.dt.bfloat16
I32 = mybir.dt.int32
AF = mybir.ActivationFunctionType
ALU = mybir.AluOpType
AX = mybir.AxisListType

D = 2048
KC = 16
T = 512
NBLK = 4
DFF = 5632
NFC = 44
INW = 4360
EPS = 1e-6
NEG = -30000.0
QS = 128.0 ** -0.5
FFN_GROUPS = [(0, 12), (12, 24), (24, 34), (34, 44)]
NSLOT = 4
import os
STOP = float(os.environ.get('KSTOP', '9'))
TWO_PI = 2.0 * math.pi

C_IDF = 0
C_PERM = 128
C_MMASK = 256
C_SEL = 768
C_INVF = 1280
NCONST = 1281
C_AMASK = 1281
NCONST_D = 1793


def make_consts():
    c = np.zeros((128, NCONST_D), np.float32)
    c[:, C_IDF:C_IDF + 128] = np.eye(128, dtype=np.float32)
    pm = np.zeros((128, 128), np.float32)
    for hb in (0, 64):
        for f in range(8):
            pm[hb + f + 8, hb + f] = -1.0
            pm[hb + f, hb + f + 8] = 1.0
    c[:, C_PERM:C_PERM + 128] = pm
    k = np.arange(128)[:, None]
    q = np.arange(128)[None, :]
    prev = np.where(k > q, 0.0, NEG).astype(np.float32)
    cur = np.where(k <= q, 0.0, NEG).astype(np.float32)
    c[:, C_AMASK:C_AMASK + 512] = np.concatenate([prev, cur, prev, cur], axis=1)
    mm = np.where(k > q, float(os.environ.get('KBIG', '60.0')), 0.0).astype(np.float32)
    c[:, C_MMASK:C_MMASK + 512] = np.concatenate([mm] * 4, axis=1)
    for h in range(4):
        c[h, C_SEL + h * 128:C_SEL + (h + 1) * 128] = 1.0
    inv = (500000.0 ** (-np.arange(0, 16, 2, dtype=np.float32) / 16.0)).astype(np.float32)
    for p in range(128):
        f = p % 64
        c[p, C_INVF] = inv[f % 8] if f < 16 else 0.0
    return c


class Prog:
    def __init__(self):
        self.ins = []
        self.lastw = {}
        self.readers = {}
        self.dma_count = {}
        self.alias = {}
        self.last_any = {}

    def _x(self, keys):
        out = []
        for k in keys:
            a = self.alias.get(k)
            if a is None:
                out.append(k)
            else:
                out.extend(a)
        return out

    def add(self, eng, fn, r=(), w=(), dma=None, w_nodep=()):
        r, w, w_nodep = self._x(r), self._x(w), self._x(w_nodep)
        def bank(k):
            return k[:3] if (len(k) >= 3 and k[:2] == "ps" and k[2].isdigit()) else k
        r = list(dict.fromkeys(bank(k) for k in r))
        w = list(dict.fromkeys(bank(k) for k in w))
        i = len(self.ins)
        deps = set()
        for k in set(r) | set(w):
            if len(k) == 3 and k[:2] == "ps" and k[2].isdigit():
                la = self.last_any.setdefault(k, {})
                for e2, j in la.items():
                    if e2 != eng:
                        deps.add(j)
                la[eng] = i
        for k in r:
            if k in self.lastw:
                deps.add(self.lastw[k])
        for k in w:
            if k in self.lastw:
                deps.add(self.lastw[k])
            deps.update(self.readers.get(k, ()))
        rec = dict(eng=eng, fn=fn, deps=deps, dma=dma, cnt=None, needed=False, idx=None)
        if dma is not None:
            self.dma_count[dma] = self.dma_count.get(dma, 0) + 1
            rec['cnt'] = self.dma_count[dma]
        self.ins.append(rec)
        for k in r:
            self.readers.setdefault(k, []).append(i)
        for k in list(w) + list(w_nodep):
            self.lastw[k] = i
            self.readers[k] = []
        return i

    def emit(self, nc, es):
        ins = self.ins
        for rec in ins:
            for d in rec['deps']:
                dd = ins[d]
                if dd['dma'] is None and not (dd['eng'] == 'pe' and rec['eng'] == 'pe'):
                    dd['needed'] = True
        cnt = {}
        per_eng = {}
        for rec in ins:
            per_eng.setdefault(rec['eng'], []).append(rec)
            if rec['dma'] is None and rec['needed']:
                cnt[rec['eng']] = cnt.get(rec['eng'], 0) + 1
                rec['idx'] = cnt[rec['eng']]
        esem = {e: es.enter_context(nc.semaphore("se_" + e)) for e in per_eng}
        dsem = {}
        for n, k in enumerate(self.dma_count):
            dsem[k] = es.enter_context(nc.semaphore("sd_%d" % n))
        block = es.enter_context(nc.Block())

        def run(ename, eng):
            waited = {}
            for rec in per_eng.get(ename, []):
                waits = {}
                for d in rec['deps']:
                    dd = ins[d]
                    if dd['dma'] is not None:
                        key = ('d', dd['dma'])
                        val = 16 * dd['cnt']
                    else:
                        if dd['eng'] == 'pe' and ename == 'pe':
                            continue
                        key = ('e', dd['eng'])
                        val = dd['idx']
                    if val > waits.get(key, 0):
                        waits[key] = val
                for key, val in waits.items():
                    if waited.get(key, 0) >= val:
                        continue
                    eng.wait_ge(dsem[key[1]] if key[0] == 'd' else esem[key[1]], val)
                    waited[key] = val
                if rec['fn'] is None:
                    continue
                inst = rec['fn'](eng)
                if rec['dma'] is not None:
                    inst.then_inc(dsem[rec['dma']], 16)
                elif rec['needed']:
                    inst.then_inc(esem[ename], 1)

        @block.sync
        def _(e):
            run('sp', e)

        @block.scalar
        def _(e):
            run('act', e)

        @block.vector
        def _(e):
            run('dve', e)

        @block.gpsimd
        def _(e):
            run('pool', e)

        @block.tensor
        def _(e):
            run('pe', e)


def build_program(L, NT, S_total=None):
    S = NT * T
    nc = bass.Bass("TRN2", target_bir_lowering=False)
    es = ExitStack()
    P = Prog()

    def dram(name, shape, dt, kind):
        return nc.dram_tensor(name, list(shape), dt, kind=kind)

    x_d = dram("x", [S, D], F32, "ExternalInput").ap()
    c_d = dram("c", [KC, 128], F32, "ExternalInput").ap()
    pos_d = dram("positions", [1, S], I32, "ExternalInput").ap()
    adaw_d = dram("ada_w", [L, D, 6 * D], F32, "ExternalInput").ap()
    adab_d = dram("ada_b", [L, 96, 128], F32, "ExternalInput").ap()
    nmix_d = dram("norm_mix_g", [L, KC, 128], F32, "ExternalInput").ap()
    win_d = dram("w_in", [L, D, INW], F32, "ExternalInput").ap()
    bg_d = dram("b_gates", [L, 8, 1], F32, "ExternalInput").ap()
    cqk_d = dram("conv_qk", [L, 32, 128], F32, "ExternalInput").ap()
    hg_d = dram("mlstm_head_g", [L, 8, 128], F32, "ExternalInput").ap()
    sink_d = dram("sinks", [L, 1, 16], F32, "ExternalInput").ap()
    wout_d = dram("w_out", [L, D, D], F32, "ExternalInput").ap()
    nffn_d = dram("norm_ffn_g", [L, KC, 128], F32, "ExternalInput").ap()
    wup_d = dram("w_up", [L, D, 2 * DFF], F32, "ExternalInput").ap()
    cffn_d = dram("conv_ffn", [L, 132, 128], F32, "ExternalInput").ap()
    wdn_d = dram("w_down", [L, DFF, D], F32, "ExternalInput").ap()
    fg_d = dram("final_g", [KC, 128], F32, "ExternalInput").ap()
    cst_d = dram("consts", [128, NCONST_D], F32, "ExternalInput").ap()
    out_d = dram("out", [S, D], F32, "ExternalOutput").ap()

    WI = dram("s_wi", [L, 18, 128, KC, 256], BF16, "Internal").ap()
    WO = dram("s_wo", [L, 8, 128, KC, 256], BF16, "Internal").ap()
    WU = dram("s_wu", [L, 44, 128, KC, 256], BF16, "Internal").ap()
    WD = dram("s_wd", [L, 4, 8, 128, 12, 256], BF16, "Internal").ap()

    def sb(name, shape, dt):
        return es.enter_context(nc.sbuf_tensor(name, list(shape), dt))

    def psum(name, shape, dt):
        return es.enter_context(nc.psum_tensor(name, list(shape), dt))

    xT = sb("xT", [128, KC, T], F32)
    hT = sb("hT", [128, KC, T], BF16)
    wslot = [sb("wslot%d" % i, [128, KC * 256], BF16) for i in range(NSLOT)]
    cst = sb("cst", [128, NCONST], F32)
    idb = sb("idb", [128, 128], BF16)
    onesb = sb("onesb", [128, 128], BF16)
    amaskb = sb("amaskb", [128, 512], BF16)
    sqb = [sb("sqb%d" % i, [128, T], BF16) for i in range(2)]
    ft = [sb("ft%d" % i, [128, T + 4], F32) for i in range(8)]
    modp = sb("modp", [128, L, 96], F32)
    PA = sb("PA", [128, 128], F32)
    PB1 = sb("PB1", [128, L, 128], F32)
    PB2 = sb("PB2", [128, L, 44], F32)
    pstage = sb("pstage", [128, 128], F32)
    cact = sb("cact", [128, KC], F32)
    fgc = sb("fgc", [128, KC], F32)
    esink = sb("esink", [128, L, 16], F32)
    gbias = sb("gbias", [4, L, 2], F32)
    qk_tail = sb("qk_tail", [128, L, 8, 3], F32)
    qkT = sb("qkT", [128, 8, T], BF16)
    vm = sb("vm", [128, NBLK, 4, 260], BF16)
    oT = sb("oT", [128, 8, T], BF16)
    kaT = sb("kaT", [128, 2, T + 128], BF16)
    ka_tail = sb("ka_tail", [128, L, 2, 128], BF16)
    va = sb("va", [128, NBLK + 1, 2, 68], BF16)
    va_tail = sb("va_tail", [128, L, 2, 68], BF16)
    cosT = sb("cosT", [128, T], F32)
    sinT = sb("sinT", [128, T], F32)
    g_G = sb("g_G", [4, T], F32)
    rows = sb("rows", [96, T], F32)
    gstate = sb("gstate", [4, L, 2], F32)
    gcols = sb("gcols", [128, NBLK, 96], F32)
    ecols = sb("ecols", [128, NBLK, 4], F32)
    gend = sb("gend", [128, 4, 5], F32)
    gprev = sb("gprev", [128, L, 4], F32)
    wexp = [sb("wexp%d" % i, [128, 128], F32) for i in range(2)]
    stw = [sb("stw%d" % i, [128, 128], BF16) for i in range(2)]
    mtmp = [sb("mtmp%d" % i, [128, 260], F32) for i in range(2)]
    hun = [sb("hun%d" % i, [128, 260], F32) for i in range(2)]
    sqj = sb("sqj", [128, 256], F32)
    mcol = [sb("mcol%d" % i, [128, 12], F32) for i in range(2)]
    vw = [sb("vw%d" % i, [128, 260], BF16) for i in range(2)]
    ktok = [sb("ktok%d" % i, [128, 128], BF16) for i in range(2)]
    Cst = sb("Cst", [128, L, 4, 260], F32)
    Cb = sb("Cb", [128, 4, 260], BF16)
    ytm = [sb("ytm%d" % i, [128, 256], BF16) for i in range(2)]
    yta = sb("yta", [128, 1024], BF16)
    pT = [sb("pT%d" % i, [128, 512], BF16) for i in range(2)]
    acol = [sb("acol%d" % i, [128, 8], F32) for i in range(2)]
    ffn_tail = sb("ffn_tail", [128, L, NFC, 2], F32)
    actb = sb("actb", [128, 12, T], BF16)
    qaT = actb
    xstage = hT[:].rearrange("p a b -> p (a b)").bitcast(F32)[:, 0:D]
    XS = ["h%d" % i for i in range(8)]
    rs_tmp, rstd = ft[4][:, 0:T], ft[5][:, 0:T]
    tmpn = [ft[6][:, 0:T], ft[7][:, 0:T]]
    qktmp = [ft[0][:, 0:T + 3], ft[1][:, 0:T + 3]]
    cacc = [ft[2][:, 0:T], ft[3][:, 0:T]]
    tmpq = [ft[0][:, 0:T], ft[1][:, 0:T]]
    rt1, rt2 = ft[2][:, 0:T], ft[3][:, 0:T]
    posi = ft[7][:, 0:T].bitcast(I32)
    ang = ft[6][:, 0:T]
    g_ti, g_tf, g_e, g_lfh, g_B = (ft[i][0:4, 0:T] for i in range(5))
    gbuf = [ft[0][:, 0:T + 2], ft[1][:, 0:T + 2]]
    ffacc = [ft[2][:, 0:T], ft[3][:, 0:T]]
    ffs = [ft[4][:, 0:T], ft[5][:, 0:T]]
    gbuf4 = [ft[i][:, 0:T + 2] for i in (0, 1, 4, 6)]
    ffacc4 = [ft[i][:, 0:T] for i in (2, 3, 5, 7)]
    AL = P.alias
    AL.update({"rs_tmp": ["ft4"], "rstd": ["ft5"], "tmpn0": ["ft6"], "tmpn1": ["ft7"],
               "qktmp0": ["ft0"], "qktmp0t": ["ft0"], "qktmp1": ["ft1"], "qktmp1t": ["ft1"],
               "cacc0": ["ft2"], "cacc1": ["ft3"], "tmpq0": ["ft0"], "tmpq1": ["ft1"],
               "rt1": ["ft2"], "rt2": ["ft3"], "posi": ["ft7"], "ang": ["ft6"],
               "g_ti": ["ft0"], "g_tf": ["ft1"], "g_e": ["ft2"], "g_lfh": ["ft3"], "g_B": ["ft4"],
               "gbuf0": ["ft0"], "gbuf0t": ["ft0"], "gbuf1": ["ft1"], "gbuf1t": ["ft1"],
               "gbuf2": ["ft4"], "gbuf2t": ["ft4"], "gbuf3": ["ft6"], "gbuf3t": ["ft6"],
               "ffacc0": ["ft2"], "ffacc1": ["ft3"], "ffacc2": ["ft5"], "ffacc3": ["ft7"],
               "xstage": XS})
    for i in range(8):
        AL["qaT%d" % i] = ["actb%d" % i]

    ps = [psum("ps%d" % i, [128, 512], F32) for i in range(8)]

    def pe(fn, r, w):
        P.add('pe', fn, r, w)

    def act(fn, r, w):
        P.add('act', fn, r, w)

    def dve(fn, r, w):
        P.add('dve', fn, r, w)

    def pool(fn, r, w):
        P.add('pool', fn, r, w)

    def mm(out, lhsT, rhs, start, stop, r, w):
        pe(lambda e: e.matmul(out, lhsT=lhsT, rhs=rhs, start=start, stop=stop), r, w)

    def tr(out, in_, ident, r, w):
        pe(lambda e: e.transpose(out, in_, ident), r, w)

    def A_(out, in_, func, bias=None, scale=None, r=(), w=()):
        kw = {}
        if bias is not None:
            kw['bias'] = bias
        if scale is not None:
            kw['scale'] = scale
        act(lambda e: e.activation(out, in_, func, **kw), r, w)

    def TS(out, in0, s1, s2, op0, op1=None, r=(), w=(), eng='dve'):
        if op1 is None:
            P.add(eng, lambda e: e.tensor_scalar(out, in0, s1, None, op0), r, w)
        else:
            P.add(eng, lambda e: e.tensor_scalar(out, in0, s1, s2, op0, op1), r, w)

    def TT(out, in0, in1, op, r=(), w=(), eng='dve'):
        P.add(eng, lambda e: e.tensor_tensor(out, in0, in1, op), r, w)

    def STT(out, in0, scalar, in1, op0, op1, r=(), w=()):
        dve(lambda e: e.scalar_tensor_tensor(out, in0, scalar, in1, op0, op1), r, w)

    def CP(out, in_, r=(), w=(), eng='dve'):
        if eng == 'act':
            act(lambda e: e.copy(out, in_), r, w)
        else:
            P.add(eng, lambda e: e.tensor_copy(out, in_), r, w)

    idf = cst[:, C_IDF:C_IDF + 128]
    perm = cst[:, C_PERM:C_PERM + 128]
    mmask = cst[:, C_MMASK:C_MMASK + 512]
    invf = cst[:, C_INVF:C_INVF + 1]

    slot_ctr = [0]

    def wload(src_ap, shape3, queue='sp'):
        i = slot_ctr[0] % NSLOT
        slot_ctr[0] += 1
        a, b = shape3
        view = wslot[i][:, 0:a * b].rearrange("p (a b) -> p a b", a=a)
        key = "wslot%d" % i
        P.add(queue, lambda e: e.dma_start(out=view, in_=src_ap), r=["wscr"], w=[key], dma=key)
        return view, key

    acc_ctr = [0]

    def next_acc():
        i = acc_ctr[0] % 2
        acc_ctr[0] += 1
        return ps[i], "ps%d" % i

    ev_ctr = [0]

    P.add('sp', lambda e: e.dma_start(out=cst[:], in_=cst_d[:, 0:NCONST]), w=["cst"], dma="cst")
    P.add('sp', lambda e: e.dma_start(out=ft[0][:, 0:512], in_=cst_d[:, C_AMASK:C_AMASK + 512]), w=["ft0"], dma="cst2")
    CP(idb[:], idf, r=["cst"], w=["idb"])
    dve(lambda e: e.memset(onesb[:], 1.0), (), ["onesb"])
    CP(amaskb[:], ft[0][:, 0:512], r=["ft0"], w=["amaskb"])
    pool(lambda e: e.memset(Cst[:], 0.0), (), ["Cst"])
    pool(lambda e: e.memset(qk_tail[:], 0.0), (), ["qk_tail"])
    pool(lambda e: e.memset(ka_tail[:], 0.0), (), ["ka_tail"])
    pool(lambda e: e.memset(va_tail[:], 0.0), (), ["va_tail"])
    pool(lambda e: e.memset(ffn_tail[:], 0.0), (), ["ffn_tail"])
    pool(lambda e: e.memset(gstate[:], 0.0), (), ["gstate"])
    pool(lambda e: e.memset(gprev[:], 0.0), (), ["gprev"])
    pool(lambda e: e.memset(rows[:], 0.0), (), ["rows"])
    pool(lambda e: e.memset(vm[:], 1.0), (), ["vm"])
    pool(lambda e: e.memset(va[:], 1.0), (), ["va"])

    def cast(dst, src, l):
        P.add('pool', lambda e: e.dma_start(out=dst, in_=src), r=(), w=(), dma=("cast", l),
              w_nodep=["wscr"])

    def colview(w2d, c0, n):
        return w2d[:, c0:c0 + n].rearrange("(kc p) c -> p kc c", p=128)

    for l in range(L if STOP >= 2 else 0):
        w = win_d[l]
        srcs = [0, 256, 512, 768, 1024, 1280, 1536, 1792, 2048, 2304, 2560, 2816,
                3080, 3336, 3592, 3848]
        for b, c0 in enumerate(srcs):
            cast(WI[l, b], colview(w, c0, 256), l)
        cast(WI[l, 16][:, :, 0:64], colview(w, 4104, 64), l)
        cast(WI[l, 16][:, :, 64:128], colview(w, 4104, 64), l)
        cast(WI[l, 16][:, :, 128:192], colview(w, 4168, 64), l)
        cast(WI[l, 16][:, :, 192:256], colview(w, 4168, 64), l)
        cast(WI[l, 17][:, :, 0:128], colview(w, 4232, 128), l)
        cast(WI[l, 17][:, :, 128:136], colview(w, 3072, 8), l)
        for b in range(8):
            cast(WO[l, b], colview(wout_d[l], b * 256, 256), l)
        for b in range(44):
            cast(WU[l, b], colview(wup_d[l], b * 256, 256), l)
        for g, (k0, k1) in enumerate(FFN_GROUPS):
            nk = k1 - k0
            for cb in range(8):
                src = wdn_d[l][k0 * 128:k1 * 128, cb * 256:(cb + 1) * 256].rearrange(
                    "(kc p) c -> p kc c", p=128)
                cast(WD[l, g, cb][:, 0:nk, :], src, l)

    gl32 = sb("gl32", [128, 32], F32)

    def load_T(rows_list, dst_ap, n):
        r0 = 0
        for j, (src, k) in enumerate(rows_list):
            fn = (lambda s, a, b: (lambda e: e.dma_start(out=pstage[a:b, :], in_=s)))(src, r0, r0 + k)
            if j == 0:
                P.add('sp', fn, r=(), w=["pstage"], dma="pstage")
            else:
                P.add('sp', fn, r=(), w=(), dma="pstage", w_nodep=["pstage"])
            r0 += k
        assert r0 == n
        tr(ps[2][:, 0:n], pstage[0:n, :], idf[0:n, 0:n], r=["pstage", "cst"], w=["ps2"])
        CP(dst_ap, ps[2][:, 0:n], r=["ps2"], w=["params"])

    if STOP >= 3:
      load_T([(c_d, 16), (fg_d, 16)], gl32[:, 0:32], 32)
    if STOP >= 3:
      A_(cact[:], gl32[:, 0:16], AF.Silu, r=["params"], w=["cact"])
      CP(fgc[:], gl32[:, 16:32], r=["params"], w=["fgc"])
    def adaln(l):
        for j in range(96):
            i = slot_ctr[0] % NSLOT
            slot_ctr[0] += 1
            view = wslot[i][:].bitcast(F32)[:, 0:KC * 128].rearrange("p (a b) -> p a b", a=KC)
            key = "wslot%d" % i
            src = adaw_d[l][:, j * 128:(j + 1) * 128].rearrange("(kc p) c -> p kc c", p=128)
            P.add('sp', (lambda v, s: (lambda e: e.dma_start(out=v, in_=s)))(view, src), r=(), w=[key], dma=key)
            for kc in range(KC):
                mm(ps[3][:, j:j + 1], view[:, kc, :], cact[:, kc:kc + 1], kc == 0, kc == KC - 1,
                   r=[key, "cact"], w=["ps3"])
        TT(modp[:, l, :], ps[3][:, 0:96], PA[:, 0:96], ALU.add, r=["ps3", "params"], w=["modp%d" % l])
        STT(modp[:, l, 16:32], modp[:, l, 16:32], 1.0, PA[:, 96:112], ALU.add, ALU.mult,
            r=["modp%d" % l, "params"], w=["modp%d" % l])
        STT(modp[:, l, 64:80], modp[:, l, 64:80], 1.0, PA[:, 112:128], ALU.add, ALU.mult,
            r=["modp%d" % l, "params"], w=["modp%d" % l])


    for l in range(L if STOP >= 3 else 0):
        load_T([(adab_d[l], 96), (nmix_d[l], 16), (nffn_d[l], 16)], PA[:, :], 128)
        load_T([(cffn_d[l][0:128, :], 128)], PB1[:, l, :], 128)
        load_T([(cffn_d[l][128:132, :], 4), (cqk_d[l], 32), (hg_d[l], 8)], PB2[:, l, :], 44)
        P.add('sp', (lambda ll: (lambda e: e.dma_start(out=esink[:, ll, :], in_=sink_d[ll].broadcast_to([128, 16]))))(l),
              r=(), w=["esink%d" % l], dma="esink%d" % l)
        A_(esink[:, l, :], esink[:, l, :], AF.Exp, r=["esink%d" % l], w=["esink%d" % l])
        P.add('sp', (lambda ll: (lambda e: e.dma_start(out=gbias[:, ll, 0:1], in_=bg_d[ll][0:4, :])))(l),
              r=(), w=["gbias%d" % l], dma="gbias%d" % l)
        P.add('sp', (lambda ll: (lambda e: e.dma_start(out=gbias[:, ll, 1:2], in_=bg_d[ll][4:8, :])))(l),
              r=(), w=(), dma="gbias%d" % l, w_nodep=["gbias%d" % l])
        TS(gbias[:, l, :], gbias[:, l, :], 1.0 / 15.0, None, ALU.mult, r=["gbias%d" % l], w=["gbias%d" % l])
        if STOP >= 4:
            adaln(l)

    def cffn_col(l, k, j):
        idx = k * NFC + j
        return PB1[:, l, idx:idx + 1] if idx < 128 else PB2[:, l, idx - 128:idx - 127]

    def cqk_col(l, k, c):
        return PB2[:, l, 4 + k * 8 + c:5 + k * 8 + c]

    def hg_col(l, c):
        return PB2[:, l, 36 + c:37 + c]

    def rms_rstd(tag):
        for kc in range(KC):
            s = sqb[kc % 2]
            A_(s[:], xT[:, kc, :], AF.Square, r=["x%d" % kc], w=["sqb%d" % (kc % 2)])
            mm(ps[2][:], onesb[:], s[:], kc == 0, kc == KC - 1, r=["sqb%d" % (kc % 2), "onesb"], w=["ps2"])
        A_(rs_tmp[:], ps[2][:], AF.Ln, bias=EPS, scale=1.0 / D, r=["ps2"], w=["rs_tmp"])
        A_(rstd[:], rs_tmp[:], AF.Exp, scale=-0.5, r=["rs_tmp"], w=["rstd"])

    def norm_mod(l, gs_off, sh_off):
        rms_rstd("n")
        for kc in range(KC):
            tn = tmpn[kc % 2]
            STT(tn[:], xT[:, kc, :], modp[:, l, gs_off + kc:gs_off + kc + 1], rstd[:], ALU.mult, ALU.mult,
                r=["x%d" % kc, "modp%d" % l, "rstd"], w=["tmpn%d" % (kc % 2)])
            A_(hT[:, kc, :], tn[:], AF.Identity, bias=modp[:, l, sh_off + kc:sh_off + kc + 1],
               r=["tmpn%d" % (kc % 2), "modp%d" % l], w=["h%d" % kc])

    hkeys = ["h%d" % kc for kc in range(KC)]

    def proj_fm(view, key, c0, ncols, out_ps, pskeys, rhs_cols=None):
        for kc in range(KC):
            mm(out_ps, view[:, kc, c0:c0 + ncols], hT[:, kc, :], kc == 0, kc == KC - 1,
               r=[key, "h%d" % kc], w=pskeys)

    def load_x_tile(t):
        for blk in range(NBLK):
            r0 = t * T + blk * 128
            P.add('pool', (lambda a: (lambda e: e.dma_start(out=xstage[:], in_=x_d[a:a + 128, :])))(r0),
                  r=(), w=["xstage"], dma="xstage")
            for q4 in range(4):
                bank = ps[4 + (q4 % 2)]
                bkey = ["ps%da" % (4 + q4 % 2), "ps%db" % (4 + q4 % 2)]
                for j in range(4):
                    kc = q4 * 4 + j
                    tr(bank[:, j * 128:(j + 1) * 128], xstage[:, kc * 128:(kc + 1) * 128], idf,
                       r=["xstage", "cst"], w=bkey)
                dst = xT[:, q4 * 4:q4 * 4 + 4, blk * 128:(blk + 1) * 128]
                src = bank[:].rearrange("p (a b) -> p a b", a=4)
                if q4 % 2 == 0:
                    CP(dst, src, r=bkey, w=["x%d" % (q4 * 4 + j) for j in range(4)])
                else:
                    CP(dst, src, r=bkey, w=["x%d" % (q4 * 4 + j) for j in range(4)], eng='act')

    def rope_tables(t):
        P.add('sp', (lambda a: (lambda e: e.dma_start(out=posi[:], in_=pos_d[:, a:a + T].broadcast_to([128, T]))))(t * T),
              r=(), w=["posi"], dma="posi")
        CP(ang[:], posi[:], r=["posi"], w=["ang"])
        TS(ang[:], ang[:], invf, None, ALU.mult, r=["ang", "cst"], w=["ang"])
        C1 = 6.28125
        C2 = TWO_PI - C1
        yv, kf, tv = ft[0][:, 0:T], ft[2][:, 0:T], ft[3][:, 0:T]
        ki = ft[1][:, 0:T].bitcast(I32)
        for tab, ph, key in ((cosT, 0.25, "cosT"), (sinT, 0.0, "sinT")):
            TS(yv, ang[:], 1.0 / TWO_PI, ph, ALU.mult, ALU.add, r=["ang"], w=["ft0"])
            CP(ki, yv, r=["ft0"], w=["ft1"])
            CP(kf, ki, r=["ft1"], w=["ft2"])
            STT(tv, kf, -C1, ang[:], ALU.mult, ALU.add, r=["ft2", "ang"], w=["ft3"])
            STT(tv, kf, -C2, tv, ALU.mult, ALU.add, r=["ft2", "ft3"], w=["ft3"])
            TS(tv, tv, ph * TWO_PI, math.pi, ALU.add, ALU.min, r=["ft3"], w=["ft3"])
            TS(tv, tv, -math.pi, None, ALU.max, r=["ft3"], w=["ft3"])
            A_(tab[:], tv, AF.Sin, r=["ft3"], w=[key])

    def final_out(t):
        rms_rstd("f")
        for blk in range(NBLK):
            for q4 in range(4):
                bank = ps[4 + (q4 % 2)]
                bkey = ["ps%da" % (4 + q4 % 2), "ps%db" % (4 + q4 % 2)]
                for j in range(4):
                    kc = q4 * 4 + j
                    tn = tmpn[kc % 2]
                    STT(tn[:, 0:128], xT[:, kc, blk * 128:(blk + 1) * 128], fgc[:, kc:kc + 1],
                        rstd[:, blk * 128:(blk + 1) * 128], ALU.mult, ALU.mult,
                        r=["x%d" % kc, "fgc", "rstd"], w=["tmpn%d" % (kc % 2)])
                    tr(bank[:, j * 128:(j + 1) * 128], tn[:, 0:128], idf, r=["tmpn%d" % (kc % 2), "cst"], w=bkey)
                CP(xstage[:, q4 * 512:(q4 + 1) * 512], bank[:], r=bkey, w=["xstage"],
                   eng=('dve' if q4 % 2 == 0 else 'act'))
            r0 = t * T + blk * 128
            P.add('pool', (lambda a: (lambda e: e.dma_start(out=out_d[a:a + 128, :], in_=xstage[:])))(r0),
                  r=["xstage"], w=["outd"], dma="outd")

    def evac_alt():
        ev_ctr[0] += 1
        return 'act' if ev_ctr[0] % 2 == 0 else 'dve'

    def mixer(l, t):
        norm_mod(l, 16, 0)
        WIl = WI[l]

        def rope_chunk(acc, akey, dst, dkeys, idx):
            tq = tmpq[idx % 2]
            tk = "tmpq%d" % (idx % 2)
            CP(tq[:], acc[:], r=[akey], w=[tk], eng='act')
            mm(ps[2][:], perm, tq[:], True, True, r=[tk, "cst"], w=["ps2"])
            TT(rt1[:], ps[2][:], sinT[:], ALU.mult, r=["ps2", "sinT"], w=["rt1"])
            TT(rt2[:], tq[:], cosT[:], ALU.mult, r=[tk, "cosT"], w=["rt2"], eng='pool')
            TT(dst, rt1[:], rt2[:], ALU.add, r=["rt1", "rt2"], w=dkeys)

        view, key = wload(WIl[16], (KC, 256))
        CP(kaT[:, :, 0:128], ka_tail[:, l, :, :], r=["ka_tail"], w=["kaT_t"], eng='pool')
        for g in range(2):
            acc, akey = next_acc()
            proj_fm(view, key, g * 128, 128, acc[:], [akey])
            rope_chunk(acc, akey, kaT[:, g, 128:T + 128], ["kaT%d" % g], g)
        view, key = wload(WIl[17][:, :, 0:136], (KC, 136))
        CP(va[:, 0, :, :], va_tail[:, l, :, :], r=["va_tail"], w=["va_t"], eng='pool')
        for blk in range(NBLK):
            acc, akey = next_acc()
            for kc in range(KC):
                mm(acc[:, 0:128], hT[:, kc, blk * 128:(blk + 1) * 128], view[:, kc, 0:128], kc == 0, kc == KC - 1,
                   r=[key, "h%d" % kc], w=[akey])
            CP(va[:, 1 + blk, :, 0:64], acc[:, 0:128].rearrange("p (g d) -> p g d", g=2), r=[akey], w=["va"],
               eng=evac_alt())
        acc_i, ki = next_acc()
        for kc in range(KC):
            mm(acc_i[0:4, :], view[:, kc, 128:132], hT[:, kc, :], kc == 0, kc == KC - 1, r=[key, "h%d" % kc], w=[ki])
        A_(g_ti[:], acc_i[0:4, :], AF.Tanh, bias=gbias[:, l, 0:1], scale=1.0 / 15.0, r=[ki, "gbias%d" % l], w=["g_ti"])
        acc_f, kf = next_acc()
        for kc in range(KC):
            mm(acc_f[0:4, :], view[:, kc, 132:136], hT[:, kc, :], kc == 0, kc == KC - 1, r=[key, "h%d" % kc], w=[kf])
        A_(g_tf[:], acc_f[0:4, :], AF.Tanh, bias=gbias[:, l, 1:2], scale=1.0 / 15.0, r=[kf, "gbias%d" % l], w=["g_tf"])
        A_(g_e[:], g_tf[:], AF.Exp, scale=-15.0, r=["g_tf"], w=["g_e"])
        A_(g_e[:], g_e[:], AF.Ln, bias=1.0, r=["g_e"], w=["g_e"])
        TS(g_lfh[:], g_e[:], -0.5, None, ALU.mult, r=["g_e"], w=["g_lfh"])
        dve(lambda e: e.tensor_tensor_scan(g_B[:], g_lfh[:], g_lfh[:], gstate[:, l, 0:1], ALU.add, ALU.add),
            ["g_lfh", "gstate"], ["g_B"])
        STT(rows[0:4, :], g_ti[:], 15.0, g_B[:], ALU.mult, ALU.subtract, r=["g_ti", "g_B"], w=["rowsA"])
        dve(lambda e: e.tensor_tensor_scan(g_G[:], rows[0:4, :], rows[0:4, :], gstate[:, l, 1:2], ALU.max, ALU.max),
            ["rowsA", "gstate"], ["g_G"])
        CP(rows[32:36, :], g_G[:], r=["g_G"], w=["rowsG"])
        STT(rows[64:68, :], g_B[:], -1.0, g_G[:], ALU.mult, ALU.subtract, r=["g_B", "g_G"], w=["rowsM"])
        A_(rows[64:68, :], rows[64:68, :], AF.Exp, r=["rowsM"], w=["rowsM"])
        CP(gstate[:, l, 0:1], g_B[:, T - 1:T], r=["g_B"], w=["gstate"])
        CP(gstate[:, l, 1:2], g_G[:, T - 1:T], r=["g_G"], w=["gstate"])
        for blk in range(NBLK):
            tr(ps[2][:, blk * 96:(blk + 1) * 96], rows[0:96, blk * 128:(blk + 1) * 128], idf[0:96, 0:96],
               r=["rowsA", "rowsG", "rowsM", "rows", "cst"], w=["ps2"])
        CP(gcols[:], ps[2][:, 0:NBLK * 96].rearrange("p (a b) -> p a b", a=NBLK), r=["ps2"], w=["gcols"])

        for b in range(4):
            view, key = wload(WIl[b], (KC, 256))
            for cc in range(2):
                c = b * 2 + cc
                acc, akey = next_acc()
                proj_fm(view, key, cc * 128, 128, acc[:], [akey])
                qt = qktmp[c % 2]
                qk_ = "qktmp%d" % (c % 2)
                CP(qt[:, 3:T + 3], acc[:], r=[akey], w=[qk_], eng='act')
                CP(qt[:, 0:3], qk_tail[:, l, c, :], r=["qk_tail"], w=[qk_ + "t"], eng='pool')
                ca = cacc[c % 2]
                ck = "cacc%d" % (c % 2)
                TS(ca[:], qt[:, 0:T], cqk_col(l, 0, c), None, ALU.mult, r=[qk_, qk_ + "t", "params"], w=[ck])
                for k in range(1, 4):
                    STT(ca[:], qt[:, k:k + T], cqk_col(l, k, c), ca[:], ALU.mult, ALU.add,
                        r=[qk_, qk_ + "t", "params", ck], w=[ck])
                CP(qk_tail[:, l, c, :], qt[:, T:T + 3], r=[qk_], w=["qk_tail"], eng='pool')
                A_(qkT[:, c, :], ca[:], AF.Silu, r=[ck], w=["qkT%d" % c])
        for h in range(4):
            view, key = wload(WIl[4 + h], (KC, 256))
            for blk in range(NBLK):
                acc, akey = next_acc()
                for kc in range(KC):
                    mm(acc[:, 0:256], hT[:, kc, blk * 128:(blk + 1) * 128], view[:, kc, :], kc == 0, kc == KC - 1,
                       r=[key, "h%d" % kc], w=[akey])
                CP(vm[:, blk, h, 0:256], acc[:, 0:256], r=[akey], w=["vm"], eng=evac_alt())

        def gen_P():
            for b in range(4):
                view, key = wload(WIl[8 + b], (KC, 256))
                for cc in range(2):
                    c = b * 2 + cc
                    acc, akey = next_acc()
                    proj_fm(view, key, cc * 128, 128, acc[:], [akey])
                    A_(oT[:, c, :], acc[:], AF.Sigmoid, r=[akey], w=["oT%d" % c])
                    yield
            for b in range(4):
                view, key = wload(WIl[12 + b], (KC, 256))
                for cc in range(2):
                    c = b * 2 + cc
                    acc, akey = next_acc()
                    proj_fm(view, key, cc * 128, 128, acc[:], [akey])
                    rope_chunk(acc, akey, qaT[:, c, :], ["qaT%d" % c], c)
                    yield

        psG, psS, psI, psC, psU = ps[3], ps[4], ps[5], ps[6], ps[7]
        psKT = ps[4][:].bitcast(BF16)
        psYT = ps[7][:].bitcast(BF16)

        def gen_M():
            for h in range(4):
                CP(gend[:, h, 0:1], gprev[:, l, h:h + 1], r=["gprev"], w=["gend%d" % h], eng='pool')
                CP(Cb[:, h, :], Cst[:, l, h, :], r=["Cst%d" % h, "Cst"], w=["Cb%d" % h], eng='act')

            def ctx(s_):
                c, h = divmod(s_, 4)
                i2 = s_ % 2
                cs = slice(c * 128, (c + 1) * 128)
                return c, h, i2, cs, qkT[:, h, cs], qkT[:, 4 + h, cs], mcol[i2], "mcol%d" % i2

            def stageA(s_):
                c, h, i2, cs, qTc, kTc, mc, mk = ctx(s_)
                pg = psG[:, h * 128:(h + 1) * 128]
                mm(pg, cst[0:4, C_SEL + h * 128:C_SEL + (h + 1) * 128], g_G[:, cs], True, False,
                   r=["cst", "g_G"], w=["ps3"])
                mm(pg, idf, mmask[:, 0:128], False, True, r=["cst"], w=["ps3"])
                mm(psS[:, 0:128], kTc, qTc, True, True, r=["qkT%d" % h, "qkT%d" % (4 + h)], w=["ps4"])
                CP(gend[:, h, c + 1:c + 2], pg[:, 127:128], r=["ps3"], w=["gend%d" % h])
                A_(wexp[i2][:], pg, AF.Exp, bias=gcols[:, c, h:h + 1], scale=-1.0,
                   r=["ps3", "gcols"], w=["wexp%d" % i2])
                STT(stw[i2][:], psS[:, 0:128], QS, wexp[i2][:], ALU.mult, ALU.mult,
                    r=["ps4", "wexp%d" % i2], w=["stw%d" % i2])
                A_(mc[:, 0:1], gcols[:, c, 32 + h:33 + h], AF.Exp, bias=gend[:, h, c:c + 1], scale=-1.0,
                   r=["gcols", "gend%d" % h], w=[mk + "a"])
                A_(mc[:, 9:10], gend[:, h, c + 1:c + 2], AF.Exp, bias=gcols[:, c, h:h + 1], scale=-1.0,
                   r=["gend%d" % h, "gcols"], w=[mk + "a"])
                A_(mc[:, 10:11], gend[:, h, c + 1:c + 2], AF.Exp, bias=gend[:, h, c:c + 1], scale=-1.0,
                   r=["gend%d" % h], w=[mk + "a"])
                A_(vw[i2][:], vm[:, c, h, :], AF.Identity, scale=mc[:, 9:10], r=["vm", mk + "a"], w=["vw%d" % i2])

            def stageB(s_):
                c, h, i2, cs, qTc, kTc, mc, mk = ctx(s_)
                mm(psI[:, 0:260], stw[i2][:], vm[:, c, h, :], True, True, r=["stw%d" % i2, "vm"], w=["ps5"])
                mm(psC[:, 0:260], qTc, Cb[:, h, :], True, True, r=["qkT%d" % h, "Cb%d" % h], w=["ps6"])
                tr(psKT[:, 512:640], kTc, idb[:], r=["qkT%d" % (4 + h), "idb"], w=["ps4"])
                CP(ktok[i2][:], psKT[:, 512:640], r=["ps4"], w=["ktok%d" % i2])
                TS(mtmp[i2][:], psC[:, 0:260], mc[:, 0:1], QS, ALU.mult, ALU.mult, r=["ps6", mk + "a"], w=["mtmp%d" % i2])
                TT(hun[i2][:], mtmp[i2][:], psI[:, 0:260], ALU.add, r=["mtmp%d" % i2, "ps5"], w=["hun%d" % i2])
                STT(mc[:, 11:12], hun[i2][:, 256:257], -1.0, hun[i2][:, 256:257], ALU.mult, ALU.max,
                    r=["hun%d" % i2], w=[mk])
                TT(mc[:, 1:2], mc[:, 11:12], gcols[:, c, 64 + h:65 + h], ALU.max, r=[mk, "gcols"], w=[mk])
                dve(lambda e, a=mc[:, 2:3], b=mc[:, 1:2]: e.reciprocal(a, b), [mk], [mk])
                TT(sqj[:], hun[i2][:, 0:256], hun[i2][:, 0:256], ALU.mult, r=["hun%d" % i2], w=["sqj"])
                dve(lambda e, o=mc[:, 3:4], a=sqj[:]: e.tensor_reduce(o, a, AX.X, ALU.add), ["sqj"], [mk])
                STT(mc[:, 4:5], mc[:, 2:3], 1.0 / 256.0, mc[:, 2:3], ALU.mult, ALU.mult, r=[mk], w=[mk])
                TT(mc[:, 5:6], mc[:, 4:5], mc[:, 3:4], ALU.mult, r=[mk], w=[mk])
                A_(mc[:, 6:7], mc[:, 5:6], AF.Ln, bias=EPS, r=[mk], w=[mk])
                A_(mc[:, 7:8], mc[:, 6:7], AF.Exp, scale=-0.5, r=[mk], w=[mk])
                TT(mc[:, 8:9], mc[:, 7:8], mc[:, 2:3], ALU.mult, r=[mk], w=[mk])
                TS(ytm[i2][:], hun[i2][:, 0:256], mc[:, 8:9], None, ALU.mult, r=["hun%d" % i2, mk], w=["ytm%d" % i2])

            def stageC(s_):
                c, h, i2, cs, qTc, kTc, mc, mk = ctx(s_)
                mm(psU[:, 0:260], ktok[i2][:], vw[i2][:], True, True, r=["ktok%d" % i2, "vw%d" % i2], w=["ps7"])
                STT(Cst[:, l, h, :], Cst[:, l, h, :], mc[:, 10:11], psU[:, 0:260], ALU.mult, ALU.add,
                    r=["Cst%d" % h, "Cst", mk + "a", "ps7"], w=["Cst%d" % h])
                CP(Cb[:, h, :], Cst[:, l, h, :], r=["Cst%d" % h], w=["Cb%d" % h], eng='act')
                for j in range(2):
                    fc = h * 2 + j
                    tr(psYT[:, 768 + j * 128:768 + (j + 1) * 128], ytm[i2][:, j * 128:(j + 1) * 128], idb[:],
                       r=["ytm%d" % i2, "idb"], w=["ps7"])
                    STT(oT[:, fc, cs], psYT[:, 768 + j * 128:768 + (j + 1) * 128], hg_col(l, fc), oT[:, fc, cs],
                        ALU.mult, ALU.mult, r=["ps7", "params", "oT%d" % fc], w=["oT%d" % fc])

            NS = 4 * NBLK
            for k in range(NS + 2):
                if 0 <= k - 2 < NS:
                    stageC(k - 2)
                if 0 <= k - 1 < NS:
                    stageB(k - 1)
                if k < NS:
                    stageA(k)
                yield
            for h in range(4):
                CP(gprev[:, l, h:h + 1], gend[:, h, 4:5], r=["gend%d" % h], w=["gprev"], eng='pool')

        def gen_A():
            pv = ps[2]
            pvk = ["ps2"]
            psT2 = ps[2][:].bitcast(BF16)
            scs = [ps[0], ps[1]]
            scks = [["ps0"], ["ps1"]]
            for n in range(NBLK):
                qs = slice(n * 128, (n + 1) * 128)
                for hq in range(4):
                    for hh in range(2):
                        mm(scs[hh][:], idb[:], amaskb[:], True, False, r=["idb", "amaskb"], w=scks[hh])
                    for pr in range(2):
                        ci = hq * 2 + pr
                        g = ci // 4
                        for hh in range(2):
                            pbase = hh * 64
                            for kb in range(2):
                                last = (pr == 1 and kb == 1)
                                mm(scs[hh][:, pr * 256 + kb * 128:pr * 256 + (kb + 1) * 128],
                                   kaT[pbase:pbase + 64, g, (n + kb) * 128:(n + kb + 1) * 128],
                                   qaT[pbase:pbase + 64, ci, qs], False, last,
                                   r=["kaT%d" % g, "kaT_t", "qaT%d" % ci], w=scks[hh])
                    for hh in range(2):
                        A_(pT[hh][:], scs[hh][:], AF.Exp, scale=0.125, r=scks[hh], w=["pT%d" % hh])
                    for pr in range(2):
                        g = (hq * 2 + pr) // 4
                        for hh in range(2):
                            col = (pr * 2 + hh) * 68
                            for kb in range(2):
                                mm(pv[:, col:col + 68], pT[hh][:, pr * 256 + kb * 128:pr * 256 + (kb + 1) * 128],
                                   va[:, n + kb, g, :], kb == 0, kb == 1, r=["pT%d" % hh, "va", "va_t"], w=pvk)
                    ac = acol[hq % 2]
                    ak = "acol%d" % (hq % 2)
                    pv3 = pv[:, 0:272].rearrange("p (a b) -> p a b", a=4)
                    TT(ac[:, 0:4], pv3[:, :, 64], esink[:, l, hq * 4:hq * 4 + 4], ALU.add, r=pvk + ["esink%d" % l], w=[ak])
                    dve(lambda e, a=ac[:, 4:8], b=ac[:, 0:4]: e.reciprocal(a, b), [ak], [ak])
                    TT(yta[:, hq * 256:(hq + 1) * 256].rearrange("p (a b) -> p a b", a=4), pv3[:, :, 0:64],
                       ac[:, 4:8].unsqueeze(2).broadcast_to([128, 4, 64]), ALU.mult, r=pvk + [ak], w=["yta"])
                    yield
                for j in range(8):
                    reg = 768 + (j % 2) * 128
                    tr(psT2[:, reg:reg + 128], yta[:, j * 128:(j + 1) * 128], idb[:], r=["yta", "idb"], w=["ps2"])
                    CP(hT[:, 8 + j, qs], psT2[:, reg:reg + 128], r=["ps2"], w=["h%d" % (8 + j)], eng=evac_alt())
                yield

        def step(g):
            try:
                next(g)
                return True
            except StopIteration:
                return False

        Pg, Mg, Ag = gen_P(), gen_M(), gen_A()
        for _ in range(8):
            step(Pg)
        p_ok, m_ok = True, True
        while p_ok:
            if m_ok:
                m_ok = step(Mg)
            p_ok = step(Pg)
        a_ok = True
        while m_ok or a_ok:
            if m_ok:
                m_ok = step(Mg)
            if a_ok:
                a_ok = step(Ag)
        CP(ka_tail[:, l, :, :], kaT[:, :, T:T + 128], r=["kaT0", "kaT1"], w=["ka_tail"], eng='pool')
        CP(va_tail[:, l, :, :], va[:, NBLK, :, :], r=["va"], w=["va_tail"], eng='pool')

        for b in range(8):
            view, key = wload(WO[l, b], (KC, 256))
            for cc in range(2):
                oc = b * 2 + cc
                acc, akey = next_acc()
                for kc in range(KC):
                    rhs = oT[:, kc, :] if kc < 8 else hT[:, kc, :]
                    rk = ("oT%d" % kc) if kc < 8 else ("h%d" % kc)
                    mm(acc[:], view[:, kc, cc * 128:(cc + 1) * 128], rhs, kc == 0, kc == KC - 1,
                       r=[key, rk], w=[akey])
                STT(xT[:, oc, :], acc[:], modp[:, l, 32 + oc:33 + oc], xT[:, oc, :], ALU.mult, ALU.add,
                    r=[akey, "modp%d" % l, "x%d" % oc], w=["x%d" % oc])

    def ffn(l, t):
        norm_mod(l, 64, 48)
        bank_i = [0]

        def nb():
            i = bank_i[0] % 8
            bank_i[0] += 1
            keys = ["ps%d" % i] if i not in (4, 7) else ["ps%da" % i, "ps%db" % i]
            return ps[i], keys

        for g, (k0, k1) in enumerate(FFN_GROUPS):
            nk = k1 - k0
            for bb in range(k0 // 2, k1 // 2):
                gv, gk = wload(WU[l, bb], (KC, 256))
                uv, uk = wload(WU[l, 22 + bb], (KC, 256))
                for cc in range(2):
                    j = bb * 2 + cc
                    i2 = j % 2
                    pg, pgk = nb()
                    proj_fm(gv, gk, cc * 128, 128, pg[:], pgk)
                    pu, puk = nb()
                    proj_fm(uv, uk, cc * 128, 128, pu[:], puk)
                    if STOP < 8.2:
                        continue
                    i4 = j % 4
                    gb = gbuf4[i4]
                    gbk = "gbuf%d" % i4
                    fa = ffacc4[i4]
                    fk = "ffacc%d" % i4
                    CP(gb[:, 2:T + 2], pg[:], r=pgk, w=[gbk], eng='act')
                    A_(fa[:], pg[:], AF.Identity, scale=cffn_col(l, 2, j), r=pgk + ["params"], w=[fk])
                    CP(gb[:, 0:2], ffn_tail[:, l, j, :], r=["ffn_tail"], w=[gbk + "t"], eng='pool')
                    STT(fa[:], gb[:, 1:T + 1], cffn_col(l, 1, j), fa[:], ALU.mult, ALU.add,
                        r=[gbk, gbk + "t", "params", fk], w=[fk])
                    STT(fa[:], gb[:, 0:T], cffn_col(l, 0, j), fa[:], ALU.mult, ALU.add,
                        r=[gbk, gbk + "t", "params", fk], w=[fk])
                    CP(ffn_tail[:, l, j, :], gb[:, T:T + 2], r=[gbk], w=["ffn_tail"], eng='pool')
                    A_(fa[:], fa[:], AF.Silu, r=[fk], w=[fk])
                    TT(actb[:, j - k0, :], fa[:], pu[:], ALU.mult, r=[fk] + puk, w=["actb%d" % (j - k0)])
            for cb in range(8 if STOP >= 8.5 else 0):
                dv, dk = wload(WD[l, g, cb][:, 0:nk, :], (nk, 256))
                for cc in range(2):
                    oc = cb * 2 + cc
                    po, pok = nb()
                    for k in range(nk):
                        mm(po[:], dv[:, k, cc * 128:(cc + 1) * 128], actb[:, k, :], k == 0, k == nk - 1,
                           r=[dk, "actb%d" % k], w=pok)
                    STT(xT[:, oc, :], po[:], modp[:, l, 80 + oc:81 + oc], xT[:, oc, :], ALU.mult, ALU.add,
                        r=pok + ["modp%d" % l, "x%d" % oc], w=["x%d" % oc])

    for t in range(NT if STOP >= 5 else 0):
        load_x_tile(t)
        if STOP >= 6:
            rope_tables(t)
        for l in range(L if STOP >= 7 else 0):
            mixer(l, t)
            if STOP >= 8:
                ffn(l, t)
        if not os.environ.get('KDUMP'):
            final_out(t)
    P.add('sp', None, r=["outd"], w=())
    P.add('pool', None, r=["outd"], w=())
    P.emit(nc, es)
    es.close()
    return nc


def make_in_map(b, inp, L, S, consts):
    f = np.float32
    return {
        "x": np.ascontiguousarray(inp["x"][b, :S], dtype=f),
        "c": np.ascontiguousarray(inp["c"][b], dtype=f).reshape(KC, 128),
        "positions": np.ascontiguousarray(inp["positions"][b, :S], dtype=np.int32).reshape(1, S),
        "ada_w": np.ascontiguousarray(inp["ada_w"][:L], dtype=f),
        "ada_b": np.ascontiguousarray(inp["ada_b"][:L], dtype=f).reshape(L, 96, 128),
        "norm_mix_g": np.ascontiguousarray(inp["norm_mix_g"][:L], dtype=f).reshape(L, KC, 128),
        "w_in": np.ascontiguousarray(inp["w_in"][:L], dtype=f),
        "b_gates": np.ascontiguousarray(inp["b_gates"][:L], dtype=f).reshape(L, 8, 1),
        "conv_qk": np.ascontiguousarray(inp["conv_qk"][:L], dtype=f).reshape(L, 32, 128),
        "mlstm_head_g": np.ascontiguousarray(inp["mlstm_head_g"][:L], dtype=f).reshape(L, 8, 128),
        "sinks": np.ascontiguousarray(inp["sinks"][:L], dtype=f).reshape(L, 1, 16),
        "w_out": np.ascontiguousarray(inp["w_out"][:L], dtype=f),
        "norm_ffn_g": np.ascontiguousarray(inp["norm_ffn_g"][:L], dtype=f).reshape(L, KC, 128),
        "w_up": np.ascontiguousarray(inp["w_up"][:L], dtype=f),
        "conv_ffn": np.ascontiguousarray(inp["conv_ffn"][:L], dtype=f).reshape(L, 132, 128),
        "w_down": np.ascontiguousarray(inp["w_down"][:L], dtype=f),
        "final_g": np.ascontiguousarray(inp["final_g"], dtype=f).reshape(KC, 128),
        "consts": consts,
    }


_NC_CACHE = {}


def run_model(inp, L, S, batch_ids):
    key = (L, S)
    if key not in _NC_CACHE:
        _NC_CACHE[key] = build_program(L, S // T)
    nc = _NC_CACHE[key]
    consts = make_consts()
    in_maps = [make_in_map(b, inp, L, S, consts) for b in batch_ids]
    res = run_bass_kernel_spmd(nc, in_maps, core_ids=list(range(len(batch_ids))))
    return np.stack([r["out"] for r in res.results], axis=0)


def kernel(**inputs):
    inp = {k: np.asarray(v) for k, v in inputs.items()}
    B, S, _ = inp["x"].shape
    L = inp["ada_w"].shape[0]
    out = run_model(inp, L, S, list(range(B)))
    return out.astype(np.float32)
```
